# Optimizing a Trainium2 kernel written in Bass

```python
import math
import jax, jax.numpy as jnp
from jax import lax
import numpy as np

D_MODEL = 1024
BATCH = 8
SEQ = 4096
DEPTH = 4

D_FF = 2816
A_GROUPS = 4
A_GROUP_DIM = 128
A_WIDTH = A_GROUPS * A_GROUP_DIM
SGU_CHUNK = 128
B_HEADS = 8
HEAD_DIM = 64
B_WIDTH = B_HEADS * HEAD_DIM
MIX_WIDTH = A_WIDTH + B_WIDTH
IN_WIDTH = 2 * A_WIDTH + 3 * B_WIDTH
DILATED_BRANCHES = ((128, 1), (512, 4), (2048, 16))
HALF_STEPS = 64
ATT_BLOCK = 64
ROPE_THETA = 10000.0
POOL_WINDOWS = (2, 4, 8, 16)
POOL_GROUP_DIM = D_MODEL // len(POOL_WINDOWS)
N_EVEN = (DEPTH + 1) // 2
N_ODD = DEPTH // 2
RMS_EPS = 1e-6
NEG_BIG = -1e30

kernel_name = 'hybrid_sgu_dilated_pool_encoder'


def _rmsnorm(t, gain):
    tf = t.astype(jnp.float32)
    tf = tf * lax.rsqrt(jnp.mean(tf * tf, axis=-1, keepdims=True) + RMS_EPS)
    return (tf * gain.astype(jnp.float32)).astype(t.dtype)


def _swiglu(h, w_gate, w_up, w_down):
    return (jax.nn.silu(h @ w_gate) * (h @ w_up)) @ w_down


def _rope_tables(seq_len):
    pos = jnp.arange(seq_len, dtype=jnp.float32)
    inv_freq = ROPE_THETA ** (-jnp.arange(0, HEAD_DIM, 2, dtype=jnp.float32) / HEAD_DIM)
    ang = pos[:, None] * inv_freq[None, :]
    ang = jnp.concatenate([ang, ang], axis=-1)[:, None, :]
    return jnp.cos(ang), jnp.sin(ang)


def _apply_rope(t, cos, sin):
    half = HEAD_DIM // 2
    rot = jnp.concatenate([-t[..., half:], t[..., :half]], axis=-1)
    return (t.astype(jnp.float32) * cos + rot.astype(jnp.float32) * sin).astype(t.dtype)


def _neighbour_blocks(t, axis):
    pad = [(0, 0)] * t.ndim
    pad[axis] = (1, 1)
    tp = jnp.pad(t, pad)
    n = t.shape[axis]
    parts = [lax.slice_in_dim(tp, i, i + n, axis=axis) for i in range(3)]
    return jnp.concatenate(parts, axis=axis + 1)


def _dilated_branch(q, k, v, dil):
    bsz, nh, s, hd = q.shape
    span = dil * ATT_BLOCK
    sp = -(-s // span) * span
    pad = ((0, 0), (0, 0), (0, sp - s), (0, 0))
    length = sp // dil
    nb = length // ATT_BLOCK

    def to_residue(t):
        t = jnp.pad(t, pad).astype(jnp.float32)
        return t.reshape(bsz, nh, length, dil, hd).transpose(0, 1, 3, 2, 4)

    qr, kr, vr = to_residue(q), to_residue(k), to_residue(v)
    valid = (jnp.arange(sp) < s).reshape(length, dil).T
    qb = qr.reshape(bsz, nh, dil, nb, ATT_BLOCK, hd)
    kb = _neighbour_blocks(kr.reshape(bsz, nh, dil, nb, ATT_BLOCK, hd), 3)
    vb = _neighbour_blocks(vr.reshape(bsz, nh, dil, nb, ATT_BLOCK, hd), 3)
    validb = _neighbour_blocks(valid.reshape(dil, nb, ATT_BLOCK), 1)
    rel = jnp.arange(3 * ATT_BLOCK)[None, :] - ATT_BLOCK - jnp.arange(ATT_BLOCK)[:, None]
    band = jnp.abs(rel) <= HALF_STEPS
    mask = band[None, None] & validb[:, :, None, :]
    scores = jnp.einsum('bhrnqd,bhrnkd->bhrnqk', qb, kb) * (1.0 / math.sqrt(hd))
    scores = jnp.where(mask, scores, NEG_BIG)
    m = jnp.max(scores, axis=-1, keepdims=True)
    p = jnp.exp(scores - m)
    den = jnp.sum(p, axis=-1)
    o = jnp.einsum('bhrnqk,bhrnkd->bhrnqd', p, vb) / den[..., None]
    lse = m[..., 0] + jnp.log(den)
    o = o.reshape(bsz, nh, dil, length, hd).transpose(0, 1, 3, 2, 4).reshape(bsz, nh, sp, hd)[:, :, :s]
    lse = lse.reshape(bsz, nh, dil, length).transpose(0, 1, 3, 2).reshape(bsz, nh, sp)[:, :, :s]
    return o, lse


def _dilated_attention(q, k, v):
    outs, lses = [], []
    for window, dil in DILATED_BRANCHES:
        o, lse = _dilated_branch(q, k, v, dil)
        outs.append(o)
        lses.append(lse)
    wts = jax.nn.softmax(jnp.stack(lses, axis=0), axis=0)
    out = jnp.sum(wts[..., None] * jnp.stack(outs, axis=0), axis=0)
    return out.astype(q.dtype)


def _even_mixer(h, w_in, sgu_norm, w_spatial, b_spatial, q_norm, k_norm, w_out):
    bsz, s, _ = h.shape
    z = h @ w_in
    zu = z[..., :A_WIDTH]
    zv = z[..., A_WIDTH:2 * A_WIDTH]
    zq = z[..., 2 * A_WIDTH:2 * A_WIDTH + B_WIDTH]
    zk = z[..., 2 * A_WIDTH + B_WIDTH:2 * A_WIDTH + 2 * B_WIDTH]
    za = z[..., 2 * A_WIDTH + 2 * B_WIDTH:]
    u = jax.nn.gelu(zu, approximate=False)
    gv = _rmsnorm(jax.nn.gelu(zv, approximate=False).reshape(bsz, s, A_GROUPS, A_GROUP_DIM), sgu_norm)
    gv = gv.reshape(bsz, s // SGU_CHUNK, SGU_CHUNK, A_GROUPS, A_GROUP_DIM)
    mixed = jnp.einsum('gts,bcsgd->bctgd', w_spatial, gv) + b_spatial.T[None, None, :, :, None]
    a_out = u * mixed.reshape(bsz, s, A_WIDTH)
    cos, sin = _rope_tables(s)
    q = _apply_rope(_rmsnorm(zq.reshape(bsz, s, B_HEADS, HEAD_DIM), q_norm), cos, sin)
    k = _apply_rope(_rmsnorm(zk.reshape(bsz, s, B_HEADS, HEAD_DIM), k_norm), cos, sin)
    va = za.reshape(bsz, s, B_HEADS, HEAD_DIM)
    b_out = _dilated_attention(q.transpose(0, 2, 1, 3), k.transpose(0, 2, 1, 3), va.transpose(0, 2, 1, 3))
    b_out = b_out.transpose(0, 2, 1, 3).reshape(bsz, s, B_WIDTH)
    return jnp.concatenate([a_out, b_out], axis=-1) @ w_out


def _pool_mixer(h, w_group, scale):
    bsz, s, d = h.shape
    hf = h.astype(jnp.float32)
    cs = jnp.concatenate([jnp.zeros((bsz, 1, d), jnp.float32), lax.cumsum(hf, axis=1)], axis=1)
    idx = jnp.arange(s)
    outs = []
    for g, win in enumerate(POOL_WINDOWS):
        lo = win // 2
        hi = win - 1 - lo
        start = jnp.clip(idx - lo, 0, s)
        end = jnp.clip(idx + hi + 1, 0, s)
        sl = slice(g * POOL_GROUP_DIM, (g + 1) * POOL_GROUP_DIM)
        csg = cs[..., sl]
        cnt = (end - start).astype(jnp.float32)[None, :, None]
        pooled = (csg[:, end] - csg[:, start]) / cnt
        y = (pooled - hf[..., sl]).astype(h.dtype)
        outs.append(y @ w_group[g])
    return jnp.concatenate(outs, axis=-1) * scale


def setup_inputs(seed: int = 0) -> dict:
    key = jax.random.key(seed)
    ks = jax.random.split(key, 20)

    def nrm(k, shape, sc):
        return jax.random.normal(k, shape, jnp.float32) * sc

    def gain(k, shape):
        return 1.0 + 0.02 * jax.random.normal(k, shape, jnp.float32)

    return {
        'x': nrm(ks[0], (BATCH, SEQ, D_MODEL), 1.0),
        'ffn1_norm': gain(ks[1], (DEPTH, D_MODEL)),
        'ffn1_w_gate': nrm(ks[2], (DEPTH, D_MODEL, D_FF), D_MODEL ** -0.5),
        'ffn1_w_up': nrm(ks[3], (DEPTH, D_MODEL, D_FF), D_MODEL ** -0.5),
        'ffn1_w_down': nrm(ks[4], (DEPTH, D_FF, D_MODEL), D_FF ** -0.5),
        'mix_norm': gain(ks[5], (DEPTH, D_MODEL)),
        'even_w_in': nrm(ks[6], (N_EVEN, D_MODEL, IN_WIDTH), D_MODEL ** -0.5),
        'sgu_norm': gain(ks[7], (N_EVEN, A_GROUPS, A_GROUP_DIM)),
        'sgu_w_spatial': nrm(ks[8], (N_EVEN, A_GROUPS, SGU_CHUNK, SGU_CHUNK), SGU_CHUNK ** -0.5),
        'sgu_b_spatial': 1.0 + 0.1 * jax.random.normal(ks[9], (N_EVEN, A_GROUPS, SGU_CHUNK), jnp.float32),
        'attn_q_norm': gain(ks[10], (N_EVEN, HEAD_DIM)),
        'attn_k_norm': gain(ks[11], (N_EVEN, HEAD_DIM)),
        'even_w_out': nrm(ks[12], (N_EVEN, MIX_WIDTH, D_MODEL), MIX_WIDTH ** -0.5),
        'pool_w_group': nrm(ks[13], (N_ODD, len(POOL_WINDOWS), POOL_GROUP_DIM, POOL_GROUP_DIM), POOL_GROUP_DIM ** -0.5),
        'pool_scale': gain(ks[14], (N_ODD, D_MODEL)),
        'ffn2_norm': gain(ks[15], (DEPTH, D_MODEL)),
        'ffn2_w_gate': nrm(ks[16], (DEPTH, D_MODEL, D_FF), D_MODEL ** -0.5),
        'ffn2_w_up': nrm(ks[17], (DEPTH, D_MODEL, D_FF), D_MODEL ** -0.5),
        'ffn2_w_down': nrm(ks[18], (DEPTH, D_FF, D_MODEL), D_FF ** -0.5),
    }


def reference(x, ffn1_norm, ffn1_w_gate, ffn1_w_up, ffn1_w_down, mix_norm, even_w_in, sgu_norm,
              sgu_w_spatial, sgu_b_spatial, attn_q_norm, attn_k_norm, even_w_out, pool_w_group,
              pool_scale, ffn2_norm, ffn2_w_gate, ffn2_w_up, ffn2_w_down):
    for layer in range(DEPTH):
        x = x + 0.5 * _swiglu(_rmsnorm(x, ffn1_norm[layer]), ffn1_w_gate[layer], ffn1_w_up[layer], ffn1_w_down[layer])
        h = _rmsnorm(x, mix_norm[layer])
        j = layer // 2
        if layer % 2 == 0:
            x = x + _even_mixer(h, even_w_in[j], sgu_norm[j], sgu_w_spatial[j], sgu_b_spatial[j],
                                attn_q_norm[j], attn_k_norm[j], even_w_out[j])
        else:
            x = x + _pool_mixer(h, pool_w_group[j], pool_scale[j])
        x = x + 0.5 * _swiglu(_rmsnorm(x, ffn2_norm[layer]), ffn2_w_gate[layer], ffn2_w_up[layer], ffn2_w_down[layer])
    return x
```

```python
import contextlib
import math
import numpy as np
import ml_dtypes
import concourse.bass as bass
import concourse.mybir as mybir
from concourse.bass_utils import run_bass_kernel_spmd

F32 = mybir.dt.float32
BF16 = mybir.dt.bfloat16
AF = mybir.ActivationFunctionType
ALU = mybir.AluOpType
AX = mybir.AxisListType

S = 4096
D = 1024
DFF = 2816
NFC = DFF // 128
DEPTH = 4
EPS = 1e-6
NCORES = 8
INW = 2560
VH = 80
VW = 8 * VH
OH = 66
OW = 8 * OH
import os as _os
DEBUG_STOP = _os.environ.get("KSTOP")


class Sem:
    def __init__(self, h):
        self.h = h
        self.v = 0


class Arena:
    def __init__(self, ap, nwords):
        self.ap = ap
        self.n = nwords
        self.off = 0
        self.base = 0

    def _take(self, words):
        o = self.off
        self.off += words
        assert self.off <= self.n, f"arena overflow {self.off} > {self.n}"
        self.hw = max(getattr(self, "hw", 0), self.off)
        return o

    def f32(self, shape):
        n = int(np.prod(shape))
        o = self._take(n)
        v = self.ap[:, o:o + n]
        return self._shape(v, shape)

    def bf16(self, shape):
        n = int(np.prod(shape))
        w = (n + 1) // 2
        o = self._take(w)
        v = self.ap[:, o:o + w].bitcast(BF16)
        if 2 * w != n:
            v = v[:, 0:n]
        return self._shape(v, shape)

    @staticmethod
    def _shape(v, shape):
        if len(shape) == 1:
            return v
        if len(shape) == 2:
            return v.rearrange("p (a b) -> p a b", a=shape[0], b=shape[1])
        if len(shape) == 3:
            return v.rearrange("p (a b c) -> p a b c", a=shape[0], b=shape[1], c=shape[2])
        raise ValueError

    def freeze(self):
        self.base = self.off

    def reset(self):
        self.off = self.base

    def push(self):
        self.stk = getattr(self, "stk", [])
        self.stk.append(self.base)
        self.base = self.off

    def pop(self):
        self.base = self.stk.pop()
        self.off = self.base


class Ring:
    def __init__(self, bufs):
        self.bufs = bufs
        self.n = len(bufs)
        self.i = 0
        self.free = []

    def acquire(self):
        i = self.i
        self.i += 1
        self.free.append(None)
        w = []
        if i >= self.n:
            w = self.free[i - self.n]
            assert w is not None, "ring slot reused before consumer recorded"
        return i, i % self.n, self.bufs[i % self.n], list(w)

    def release(self, i, waits):
        self.free[i] = [w for w in waits if w is not None]


class TB:
    def __init__(self, ap):
        self.ap = ap
        self.w = None
        self.rd = {}


def pipeline(nitems, stages):
    ns = len(stages)
    for t in range(nitems + ns - 1):
        for sidx in reversed(range(ns)):
            i = t - sidx
            if 0 <= i < nitems:
                stages[sidx](i)


class KB:
    ENG = ("pe", "act", "dve", "pool", "sp")

    def __init__(self, nc, stack):
        self.nc = nc
        self.stack = stack
        self.q = {e: [] for e in self.ENG}
        self.waited = {e: {} for e in self.ENG}
        self.nsem = 0
        self.pool = []
        self.phase_sems = []
        self.log = {e: [] for e in self.ENG}

    def check_deadlock(self):
        val = {}
        pc = {e: 0 for e in self.ENG}
        prog = True
        while prog:
            prog = False
            for e in self.ENG:
                while pc[e] < len(self.log[e]):
                    waits, inc, _ = self.log[e][pc[e]]
                    if all(val.get(sid, 0) >= v for sid, v in waits):
                        if inc is not None and inc[1] == "clear":
                            for sid in inc[0]:
                                val[sid] = 0
                        elif inc is not None:
                            val[inc[0]] = val.get(inc[0], 0) + inc[1]
                        pc[e] += 1
                        prog = True
                    else:
                        break
        stuck = {e: (pc[e], len(self.log[e])) for e in self.ENG if pc[e] < len(self.log[e])}
        return stuck, val

    def sem(self, name=None, persistent=False):
        if not persistent and self.pool:
            sm = self.pool.pop()
        else:
            self.nsem += 1
            h = self.stack.enter_context(self.nc.semaphore(f"s{self.nsem}_{name or ''}"))
            sm = Sem(h)
        if not persistent:
            self.phase_sems.append(sm)
        return sm

    def end_phase(self):
        if not _os.environ.get("KNOPOOL"):
            self.pool.extend(self.phase_sems)
        self.phase_sems = []

    def op(self, eng, fn, waits=(), inc=None, n=None):
        ws = []
        for w in waits:
            if w is None:
                continue
            s, v = w
            if v <= 0:
                continue
            if self.waited[eng].get(id(s), 0) >= v:
                continue
            self.waited[eng][id(s)] = v
            ws.append((s.h, v))
        amt = None
        if inc is not None:
            amt = n if n is not None else 1
            inc.v += amt
        h = inc.h if inc is not None else None

        def run(e, ws=ws, fn=fn, h=h, amt=amt):
            for sh, v in ws:
                e.wait_ge(sh, v)
            ins = fn(e)
            if h is not None:
                ins.then_inc(h, amt)

        self.q[eng].append(run)
        self.log[eng].append(([(id(s_), v_) for (s_, v_) in [(w[0], w[1]) for w in waits if w is not None] if v_ > 0],
                              (id(inc), amt) if inc is not None else None, len(self.log[eng])))
        if inc is not None:
            return (inc, inc.v)
        return None

    def eng_sems(self):
        self.es = {e: self.sem("es_" + e) for e in ("pe", "act", "dve", "pool")}

    def x(self, eng, fns, reads=(), writes=(), waits=()):
        if callable(fns):
            fns = [fns]
        ws = [w for w in waits if w is not None]
        for b in reads:
            ws.append(b.w)
        for b in writes:
            ws.append(b.w)
            ws.extend(b.rd.values())
        res = None
        for k, fn in enumerate(fns):
            last = k == len(fns) - 1
            res = self.op(eng, fn, waits=ws if k == 0 else (), inc=self.es[eng] if last else None)
        for b in reads:
            b.rd[eng] = res
        for b in writes:
            b.w = res
            b.rd = {}
        return res

    def xdma(self, eng, out, in_, sem, reads=(), writes=(), waits=()):
        ws = [w for w in waits if w is not None]
        for b in reads:
            ws.append(b.w)
        for b in writes:
            ws.append(b.w)
            ws.extend(b.rd.values())
        res = self.dma(eng, out, in_, waits=ws, inc=sem)
        for b in reads:
            b.rd[("dma", id(sem))] = res
        for b in writes:
            b.w = res
            b.rd = {}
        return res

    def dma(self, eng, out, in_, waits=(), inc=None):
        return self.op(eng, lambda e: e.dma_start(out=out, in_=in_), waits=waits, inc=inc, n=16)

    def wait_only(self, eng, waits):
        ws = []
        for s, v in waits:
            if v > 0 and self.waited[eng].get(id(s), 0) < v:
                self.waited[eng][id(s)] = v
                ws.append((s.h, v))

        def run(e, ws=ws):
            for sh, v in ws:
                e.wait_ge(sh, v)

        if ws:
            self.q[eng].append(run)
            self.log[eng].append(([(id(s_), v_) for (s_, v_) in waits if v_ > 0], None, len(self.log[eng])))


class Ctx:
    pass


def build_program(nphases=99, skip=0):
    nc = bass.Bass("TRN2", target_bir_lowering=False)
    stack = contextlib.ExitStack()
    kb = KB(nc, stack)
    c = Ctx()
    c.nc, c.kb = nc, kb

    def din(name, shape, dt=F32):
        return nc.dram_tensor(name, list(shape), dt, kind="ExternalInput").ap()

    def dscr(name, shape, dt):
        return nc.dram_tensor(name, list(shape), dt, kind="Internal").ap()

    I = {}
    I["x"] = din("x", [S, D])
    for p in ("ffn1", "ffn2"):
        I[p + "_norm"] = din(p + "_norm", [DEPTH, D])
        I[p + "_w_gate"] = din(p + "_w_gate", [DEPTH, D, DFF])
        I[p + "_w_up"] = din(p + "_w_up", [DEPTH, D, DFF])
        I[p + "_w_down"] = din(p + "_w_down", [DEPTH, DFF, D])
    I["mix_norm"] = din("mix_norm", [DEPTH, D])
    I["even_w_in"] = din("even_w_in", [2, D, INW])
    I["sgu_norm"] = din("sgu_norm", [2, 4, 128])
    I["sgu_w_spatial"] = din("sgu_w_spatial", [2, 4, 128, 128])
    I["sgu_b_spatial"] = din("sgu_b_spatial", [2, 4, 128])
    I["attn_q_norm"] = din("attn_q_norm", [2, 64])
    I["attn_k_norm"] = din("attn_k_norm", [2, 64])
    I["even_w_out"] = din("even_w_out", [2, D, D])
    I["pool_w_group"] = din("pool_w_group", [2, 4, 256, 256])
    I["pool_scale"] = din("pool_scale", [2, D])
    I["c_ident"] = din("c_ident", [128, 128], BF16)
    I["c_pool"] = din("c_pool", [128, 4, 5, 128], BF16)
    I["c_mask"] = din("c_mask", [128, 512], BF16)
    I["c_cos"] = din("c_cos", [S, 64])
    I["c_sin"] = din("c_sin", [S, 64])
    out = nc.dram_tensor("out", [S, D], F32, kind="ExternalOutput").ap()
    c.I, c.out = I, out

    Wb = {}
    for p in ("ffn1", "ffn2"):
        Wb[p + "_w_gate"] = dscr(p + "_wg_b", [DEPTH, D, DFF], BF16)
        Wb[p + "_w_up"] = dscr(p + "_wu_b", [DEPTH, D, DFF], BF16)
        Wb[p + "_w_down"] = dscr(p + "_wd_b", [DEPTH, DFF, D], BF16)
    Wb["even_w_in"] = dscr("win_b", [2, D, INW], BF16)
    Wb["even_w_out"] = dscr("wout_b", [2, D, D], BF16)
    Wb["pool_w_group"] = dscr("wpool_b", [2, 4, 256, 256], BF16)
    c.Wb = Wb
    c.vaug_d = dscr("vaug_d", [S + 2048 + 16, VW], BF16)
    c.o_d = dscr("o_d", [3, S + 16, OW], F32)
    c.aoT_d = dscr("aoT_d", [512, S], BF16)

    NW = 53000
    arena_t = stack.enter_context(nc.sbuf_tensor("arena", [128, NW], F32))
    psum_t = stack.enter_context(nc.psum_tensor("psum", [128, 4096], F32))
    c.A = Arena(arena_t[:], NW)
    c.ps = psum_t[:]
    c.bank = lambda i: c.ps[:, i * 512:(i + 1) * 512]

    c.ident = c.A.bf16([128])
    c.tiny = c.A.f32([8])
    c.cst = c.A.f32([8])
    c.nh = c.A.f32([16])
    c.mask2 = c.A.bf16([512])
    c.zt = c.A.bf16([VW])
    c.A.freeze()
    c.s_const = kb.sem("const", persistent=True)
    c.s_cst = kb.sem("cst", persistent=True)
    kb.op("pool", lambda e: e.memset(c.cst[:, 0:1], -0.5), inc=c.s_cst)
    kb.op("pool", lambda e: e.memset(c.cst[:, 1:2], EPS), inc=c.s_cst)
    kb.op("pool", lambda e: e.memset(c.nh, -0.5), inc=c.s_cst)
    kb.op("pool", lambda e: e.memset(c.tiny, 0.0), inc=c.s_cst)
    rz = kb.op("pool", lambda e: e.memset(c.zt, 0.0), inc=c.s_cst)
    c.cst_ready = (c.s_cst, c.s_cst.v)
    kb.dma("sp", c.mask2, I["c_mask"], inc=c.s_const)
    for a_ in range(8):
        kb.dma("sp", c.vaug_d[0:1024, :].rearrange("(p a) f -> p a f", p=128)[:, a_, :], c.zt,
               waits=[rz] if a_ == 0 else [], inc=c.s_const)
        kb.dma("sp", c.vaug_d[1024 + S:2048 + S, :].rearrange("(p a) f -> p a f", p=128)[:, a_, :], c.zt, inc=c.s_const)
    c.const_ready = (c.s_const, c.s_const.v)
    kb.dma("sp", c.ident, I["c_ident"], inc=c.s_const)
    c.const_ready = (c.s_const, c.s_const.v)

    c.wready = {}
    phases = []
    for layer in range(DEPTH):
        phases.append(("ffn", "ffn1", layer))
        phases.append(("mix", None, layer))
        phases.append(("ffn", "ffn2", layer))

    castq = []

    def add_cast(dst, src, sm, nchunk):
        rows = dst.shape[0]
        step = rows // nchunk
        for k in range(nchunk):
            castq.append((dst[k * step:(k + 1) * step], src[k * step:(k + 1) * step], sm))
            sm.v += 16
        return (sm, sm.v)

    for pi, (kind, p, layer) in enumerate(phases):
        j = layer // 2
        if pi < skip:
            continue
        if kind == "ffn":
            sm_d = kb.sem(f"castd_{pi}", persistent=True)
            sm_g = kb.sem(f"castg_{pi}", persistent=True)
            rd = add_cast(Wb[p + "_w_down"][layer], I[p + "_w_down"][layer], sm_d, 8)
            add_cast(Wb[p + "_w_gate"][layer], I[p + "_w_gate"][layer], sm_g, 8)
            rg = add_cast(Wb[p + "_w_up"][layer], I[p + "_w_up"][layer], sm_g, 8)
            c.wready[pi] = ([rg] * (NFC // 2), rd)
        else:
            sm = kb.sem(f"cast_{pi}", persistent=True)
            if layer % 2 == 0:
                add_cast(Wb["even_w_in"][j], I["even_w_in"][j], sm, 8)
                c.wready[pi] = add_cast(Wb["even_w_out"][j], I["even_w_out"][j], sm, 8)
            else:
                c.wready[pi] = add_cast(Wb["pool_w_group"][j].rearrange("g d e -> (g d) e"),
                                        I["pool_w_group"][j].rearrange("g d e -> (g d) e"), sm, 2)
    st_cast = {"i": 0}

    def cast_pump(n):
        while n > 0 and st_cast["i"] < len(castq):
            dst, src, sm = castq[st_cast["i"]]
            st_cast["i"] += 1
            n -= 1
            h_ = sm.h
            kb.q["pool"].append(lambda e, dst=dst, src=src, h_=h_: e.dma_start(out=dst, in_=src).then_inc(h_, 16))
            kb.log["pool"].append(([], (id(sm), 16), len(kb.log["pool"])))

    c.cast_pump = cast_pump
    if skip > 0:
        cast_pump(len(castq))

    c.bar = kb.sem("bar", persistent=True)
    c.nbar = 0

    src = I["x"]
    if skip > 0:
        s_cp = kb.sem("cp")
        r_cp = kb.dma("sp", out, I["x"], inc=s_cp)
        kb.wait_only("sp", [r_cp])
        barrier(c)
        src = out
    for pi in range(skip, min(nphases, len(phases))):
        kind, p, layer = phases[pi]
        c.pi = pi
        if kind == "ffn":
            ffn_phase(c, src, out, I[p + "_norm"][layer], Wb[p + "_w_gate"][layer], Wb[p + "_w_up"][layer],
                      Wb[p + "_w_down"][layer], c.wready[pi])
            src = out
        elif layer % 2 == 1:
            if _os.environ.get("KOLDPOOL"):
                pool_phase(c, out, layer)
            else:
                pool_pipelined(c, out, layer)
        else:
            even_phase(c, out, layer)
        barrier(c)

    stuck, _ = kb.check_deadlock()
    if stuck:
        raise RuntimeError(f"deadlock in recorded program: {stuck}")
    print("arena high water (words)", c.A.hw, "of", NW)
    print("instr counts", {e: len(kb.q[e]) for e in KB.ENG}, "sems", kb.nsem)
    with nc.Block() as block:
        @block.tensor
        def _(e):
            for f in kb.q["pe"]:
                f(e)

        @block.scalar
        def _(e):
            for f in kb.q["act"]:
                f(e)

        @block.vector
        def _(e):
            for f in kb.q["dve"]:
                f(e)

        @block.gpsimd
        def _(e):
            for f in kb.q["pool"]:
                f(e)

        @block.sync
        def _(e):
            for f in kb.q["sp"]:
                f(e)
    stack.close()
    return nc


def barrier(c):
    kb = c.kb
    t = c.tiny
    for eng in KB.ENG:
        for sm in kb.phase_sems:
            kb.wait_only(eng, [(sm, sm.v)])
    if _os.environ.get("KDRAIN"):
        for eng in KB.ENG:
            kb.q[eng].append(lambda e: e.drain())
            kb.log[eng].append(([], None, len(kb.log[eng])))
    kb.op("act", lambda e: e.activation(out=t[:, 0:1], in_=t[:, 1:2], func=AF.Copy), waits=[c.cst_ready], inc=c.bar)
    kb.op("dve", lambda e: e.tensor_copy(out=t[:, 2:3], in_=t[:, 3:4]), waits=[c.cst_ready], inc=c.bar)
    kb.op("pool", lambda e: e.tensor_copy(out=t[:, 4:5], in_=t[:, 5:6]), waits=[c.cst_ready], inc=c.bar)
    kb.op("pe", lambda e: e.matmul(c.ps[0:1, 4095:4096], c.ident[0:1, 0:1], c.ident[0:1, 0:1], start=True, stop=True),
          waits=[c.const_ready], inc=c.bar)
    kb.op("sp", lambda e: e.sem_inc(c.bar.h, 1))
    c.bar.v += 1
    w_, i_, n_ = kb.log["sp"][-1]
    kb.log["sp"][-1] = (w_, (id(c.bar), 1), n_)
    for eng in KB.ENG:
        kb.wait_only(eng, [(c.bar, c.bar.v)])
    sems = list(kb.phase_sems)
    if sems:
        finals = [(sm.h, sm.v) for sm in sems if sm.v > 0]

        def clr(e, sems=sems, finals=finals):
            for h_, v_ in finals:
                e.wait_ge(h_, v_)
            for sm in sems:
                e.sem_clear(sm.h)
            e.sem_inc(c.bar.h, 1)
        kb.q["sp"].append(clr)
        kb.log["sp"].append(([], ([id(sm) for sm in sems], "clear"), len(kb.log["sp"])))
        kb.log["sp"].append(([], (id(c.bar), 1), len(kb.log["sp"])))
        c.bar.v += 1
        for eng in KB.ENG:
            kb.wait_only(eng, [(c.bar, c.bar.v)])
        for sm in sems:
            sm.v = 0
            for eng in KB.ENG:
                kb.waited[eng].pop(id(sm), None)
    kb.end_phase()


def ffn_phase(c, x_src, x_dst, gain_ap, wg_b, wu_b, wd_b, wready):
    kb, A = c.kb, c.A
    A.reset()
    T = 1024
    NT = S // T
    NBLK = T // 128
    NG = NFC // 2
    gain_bc = A.f32([D])
    xblk = [A.f32([D]) for _ in range(4)]
    ss = A.f32([4])
    rstd = A.f32([4])
    mse = A.f32([4])
    sg = [A.f32([512]) for _ in range(2)]
    sqjunk = A.bf16([D])
    hblk = [A.bf16([D]) for _ in range(2)]
    hT = [A.bf16([8, T]) for _ in range(2)]
    wg = [A.bf16([8, 256]) for _ in range(3)]
    wu = [A.bf16([8, 256]) for _ in range(3)]
    wd = A.bf16([NFC, D])
    aT = A.bf16([NFC, T])
    tp = c.bank(0).bitcast(BF16)[:, 0:1024].rearrange("p (a b) -> p a b", a=8, b=128)

    s_xload = [kb.sem("xload") for _ in range(4)]
    s_xstore = [kb.sem("xstore") for _ in range(4)]
    s_wload = [kb.sem("wload") for _ in range(3)]
    s_misc = kb.sem("misc")
    s_sq, s_rstd, s_h, s_tp, s_tpc = (kb.sem(n) for n in ("sq", "rstd", "h", "tp", "tpc"))
    s_mse = kb.sem("mse")
    s_gu, s_sg, s_a, s_dn, s_y = (kb.sem(n) for n in ("gu", "sg", "a", "dn", "y"))

    kb.dma("sp", gain_bc, gain_ap.partition_broadcast(128), inc=s_misc)
    wd_src = wd_b.rearrange("(c p) d -> p c d", p=128)
    wr_groups, wr_d = wready
    gain_ready = (s_misc, s_misc.v)
    s_wd = kb.sem("wd")
    wd_ready = (s_wd, 32)

    def load_wd():
        kb.dma("sp", wd[:, 0:11, :], wd_src[:, 0:11, :], waits=[wr_d], inc=s_wd)
        kb.dma("sp", wd[:, 11:22, :], wd_src[:, 11:22, :], inc=s_wd)
    misc_ready = gain_ready

    st = Ctx()
    st.xi = 0
    st.xfree = []
    st.xitems = {}
    st.wi = 0
    st.wlast = []
    st.nrm = 0
    st.gu = 0
    st.dn = 0
    st.tpc_tile = {}

    def load_x(key, rows, src):
        i = st.xi
        st.xi += 1
        slot = i % 4
        waits = [st.xfree[i - 4]] if i >= 4 else []
        r = kb.dma("sp", xblk[slot], src[rows * 128:(rows + 1) * 128, :], waits=waits, inc=s_xload[slot])
        st.xitems[key] = (i, slot, r)
        st.xfree.append(None)

    wg_src = wg_b.rearrange("(c p) f -> p c f", p=128)
    wu_src = wu_b.rearrange("(c p) f -> p c f", p=128)

    fast_start = (c.pi == 0)
    if fast_start:
        stg = [A.f32([8, 256]) for _ in range(3)]
        s_stg = [kb.sem("stg") for _ in range(3)]
        s_castA, s_castD = kb.sem("castA"), kb.sem("castD")
        st.stg_i = 0
        st.stg_free = []
        w32 = [c.I["ffn1_w_gate"][0].rearrange("(c p) f -> p c f", p=128),
               c.I["ffn1_w_up"][0].rearrange("(c p) f -> p c f", p=128)]

    def load_w_fast(g, i, slot):
        rs = []
        war = [(s_gu, st.wlast[i - 3])] if i >= 3 else []
        for k in range(2):
            j = st.stg_i
            st.stg_i += 1
            tile_ = stg[j % 3]
            wfree = [st.stg_free[j - 3]] if j >= 3 else []
            r_l = kb.dma("sp", tile_, w32[k][:, :, g * 256:(g + 1) * 256], waits=wfree, inc=s_stg[j % 3])
            if k == 0:
                r = kb.op("act", lambda e, tile_=tile_: e.activation(out=wg[slot], in_=tile_, func=AF.Copy),
                          waits=[r_l] + war, inc=s_castA)
            else:
                r = kb.op("dve", lambda e, tile_=tile_: e.tensor_copy(out=wu[slot], in_=tile_),
                          waits=[r_l] + war, inc=s_castD)
            st.stg_free.append(r)
            rs.append(r)
        st.wlast.append(None)
        return (i, slot, rs)

    def load_w(g):
        i = st.wi
        st.wi += 1
        slot = i % 3
        if fast_start and i < 2 * NG:
            return load_w_fast(g, i, slot)
        waits = [wr_groups[g]]
        if i >= 3:
            waits.append((s_gu, st.wlast[i - 3]))
        kb.dma("sp", wg[slot], wg_src[:, :, g * 256:(g + 1) * 256], waits=waits, inc=s_wload[slot])
        r = kb.dma("sp", wu[slot], wu_src[:, :, g * 256:(g + 1) * 256], inc=s_wload[slot])
        st.wlast.append(None)
        return (i, slot, r)

    st.pend = []

    def norm_unit(tt, b):
        i = st.nrm
        st.nrm += 1
        xi, slot, xr = st.xitems[("n", tt, b)]
        si = i % 4
        hs = i % 2
        xb = xblk[slot]
        r_sq = kb.op("act", lambda e: e.activation(out=sqjunk, in_=xb, func=AF.Square, scale=1.0 / 32.0,
                                                   accum_out=ss[:, si:si + 1]),
                     waits=[xr, (s_mse, i - 3), (s_sq, i)], inc=s_sq)
        r_e = kb.op("pool", lambda e: e.tensor_tensor(out=mse[:, si:si + 1], in0=ss[:, si:si + 1],
                                                      in1=c.cst[:, 1:2], op=ALU.add),
                    waits=[r_sq, c.cst_ready, (s_rstd, i - 3)], inc=s_mse)
        r_rs = kb.op("pool", lambda e: e.tensor_tensor(out=rstd[:, si:si + 1], in0=mse[:, si:si + 1],
                                                       in1=c.cst[:, 0:1], op=ALU.pow),
                     waits=[r_e, (s_h, i - 3)], inc=s_rstd)
        r_h = kb.op("dve", lambda e: e.scalar_tensor_tensor(out=hblk[hs], in0=xb, scalar=rstd[:, si:si + 1],
                                                            in1=gain_bc, op0=ALU.mult, op1=ALU.mult),
                    waits=[r_rs, misc_ready, (s_tp, i - 1)], inc=s_h)
        st.xfree[xi] = r_h
        st.pend.append((tt, b, i, hs, r_h))

    def norm_flush(n=None):
        while st.pend and (n is None or n > 0):
            tt, b, i, hs, r_h = st.pend.pop(0)
            if n is not None:
                n -= 1
            buf = tt % 2
            for dc in range(8):
                last = dc == 7
                r_tp = kb.op("pe", lambda e, dc=dc, hs=hs: e.transpose(out=tp[:, dc, :],
                                                                       in_=hblk[hs][:, dc * 128:(dc + 1) * 128],
                                                                       identity=c.ident),
                             waits=[r_h, (s_tpc, i), c.const_ready] if dc == 0 else [],
                             inc=s_tp if last else None)
            r_c = kb.op("act", lambda e, buf=buf, b=b: e.activation(out=hT[buf][:, :, b * 128:(b + 1) * 128], in_=tp,
                                                                    func=AF.Copy),
                        waits=[r_tp], inc=s_tpc)
            st.tpc_tile[tt] = r_c

    def gu_unit(tt, fc, half, wslot, wr):
        j = st.gu
        st.gu += 1
        pb = j % 2
        gps = c.bank(1 + 2 * pb)
        ups = c.bank(2 + 2 * pb)
        buf = tt % 2
        fo = (fc % 2) * 128
        for which, (w, pst) in enumerate(((wg[wslot], gps), (wu[wslot], ups))):
            for dc in range(8):
                first = which == 0 and dc == 0
                last = which == 1 and dc == 7
                r_mm = kb.op("pe", lambda e, w=w, pst=pst, dc=dc: e.matmul(
                    pst, w[:, dc, fo:fo + 128], hT[buf][:, dc, half * 512:(half + 1) * 512],
                    start=(dc == 0), stop=(dc == 7)),
                    waits=(list(wr) if isinstance(wr, list) else [wr]) + [st.tpc_tile[tt], (s_a, j - 1)] if first else [],
                    inc=s_gu if last else None)
        r_sg = kb.op("act", lambda e: e.activation(out=sg[pb], in_=gps, func=AF.Silu),
                     waits=[r_mm, (s_a, j - 1)], inc=s_sg)
        kb.op("dve", lambda e: e.tensor_tensor(out=aT[:, fc, half * 512:(half + 1) * 512], in0=sg[pb], in1=ups,
                                               op=ALU.mult),
              waits=[r_sg, r_mm], inc=s_a)

    def dn_unit(tt, b, dh):
        j = st.dn
        st.dn += 1
        yps = c.bank(5 + j % 2)
        xi, slot, xr = st.xitems[("r", tt, b)]
        for fc in range(NFC):
            r_mm = kb.op("pe", lambda e, fc=fc: e.matmul(
                yps, aT[:, fc, b * 128:(b + 1) * 128], wd[:, fc, dh * 512:(dh + 1) * 512],
                start=(fc == 0), stop=(fc == NFC - 1)),
                waits=[(s_a, 2 * NFC * (tt + 1)), (s_y, j - 1), wd_ready] if fc == 0 else [],
                inc=s_dn if fc == NFC - 1 else None)
        xs = xblk[slot][:, dh * 512:(dh + 1) * 512]
        r_y = kb.op("dve", lambda e: e.scalar_tensor_tensor(out=xs, in0=yps, scalar=0.5, in1=xs,
                                                            op0=ALU.mult, op1=ALU.add),
                    waits=[r_mm, xr], inc=s_y)
        if dh == 1:
            gb = tt * NBLK + b
            r_st = kb.dma("sp", x_dst[gb * 128:(gb + 1) * 128, :], xblk[slot], waits=[r_y], inc=s_xstore[slot])
            st.xfree[xi] = r_st

    load_x(("n", 0, 0), 0, x_src)
    load_x(("n", 0, 1), 1, x_src)
    wq = [load_w(0), load_w(1)]
    for b in range(NBLK):
        if b + 2 < NBLK:
            load_x(("n", 0, b + 2), b + 2, x_src)
        norm_unit(0, b)
        if b >= 1:
            norm_flush(1)
    norm_flush()
    if not fast_start:
        load_wd()
    for tt in range(NT):
        nxt = tt + 1 < NT
        if nxt:
            load_x(("n", tt + 1, 0), (tt + 1) * NBLK, x_src)
        for g in range(NG):
            wi, wslot, wr = wq.pop(0)
            gg = tt * NG + g + 2
            if gg < NT * NG:
                wq.append(load_w(gg % NG))
            for fcl in range(2):
                for half in range(2):
                    gu_unit(tt, 2 * g + fcl, half, wslot, wr)
            st.wlast[wi] = st.gu
            c.cast_pump(2 if (fast_start and tt * NG + g < 12) else 1)
            if fast_start and tt == 0 and g == 5:
                load_wd()
            if nxt:
                norm_flush()
                if g < NBLK:
                    if g + 1 < NBLK:
                        load_x(("n", tt + 1, g + 1), (tt + 1) * NBLK + g + 1, x_src)
                    norm_unit(tt + 1, g)
        load_x(("r", tt, 0), tt * NBLK, x_src)
        for b in range(NBLK):
            if b + 1 < NBLK:
                load_x(("r", tt, b + 1), tt * NBLK + b + 1, x_src)
            dn_unit(tt, b, 0)
            dn_unit(tt, b, 1)
    kb.wait_only("sp", [(s, s.v) for s in s_xstore])


class NormStage:
    def __init__(self, c, gain_ap, nx=4):
        kb, A = c.kb, c.A
        self.c = c
        self.gain_bc = A.f32([D])
        self.ss = A.f32([4])
        self.mse = A.f32([4])
        self.rstd = A.f32([4])
        self.sqjunk = A.bf16([D])
        self.xring = Ring([A.f32([D]) for _ in range(nx)])
        self.s_xload = [kb.sem("nxload") for _ in range(nx)]
        self.s_sq, self.s_mse, self.s_rstd, self.s_h, self.s_g = (kb.sem(n) for n in ("nsq", "nmse", "nrstd", "nh", "ng"))
        kb.dma("sp", self.gain_bc, gain_ap.partition_broadcast(128), inc=self.s_g)
        self.g_ready = (self.s_g, self.s_g.v)
        self.i = 0

    def load(self, src_rows):
        kb = self.c.kb
        xi, slot, xb, w = self.xring.acquire()
        r = kb.dma("sp", xb, src_rows, waits=w, inc=self.s_xload[slot])
        return (xi, xb, r)

    def norm(self, xitem, hout, extra_waits=()):
        c, kb = self.c, self.c.kb
        xi, xb, xr = xitem
        i = self.i
        self.i += 1
        si = i % 4
        ss, mse, rstd = self.ss, self.mse, self.rstd
        r_sq = kb.op("act", lambda e: e.activation(out=self.sqjunk, in_=xb, func=AF.Square, scale=1.0 / 32.0,
                                                   accum_out=ss[:, si:si + 1]),
                     waits=[xr, (self.s_mse, i - 3), (self.s_sq, i)], inc=self.s_sq)
        r_e = kb.op("pool", lambda e: e.tensor_tensor(out=mse[:, si:si + 1], in0=ss[:, si:si + 1],
                                                      in1=c.cst[:, 1:2], op=ALU.add),
                    waits=[r_sq, c.cst_ready, (self.s_rstd, i - 3)], inc=self.s_mse)
        r_rs = kb.op("pool", lambda e: e.tensor_tensor(out=rstd[:, si:si + 1], in0=mse[:, si:si + 1],
                                                       in1=c.cst[:, 0:1], op=ALU.pow),
                     waits=[r_e, (self.s_h, i - 3)], inc=self.s_rstd)
        r_h = kb.op("dve", lambda e: e.scalar_tensor_tensor(out=hout, in0=xb, scalar=rstd[:, si:si + 1],
                                                            in1=self.gain_bc, op0=ALU.mult, op1=ALU.mult),
                    waits=[r_rs, self.g_ready] + list(extra_waits), inc=self.s_h)
        return r_h


def pool_pipelined(c, xd, layer):
    kb, A, I = c.kb, c.A, c.I
    j = layer // 2
    NB = S // 128
    A.reset()
    kb.eng_sems()
    NX = 9
    g = norm_setup(c, kb, A, I["mix_norm"][layer], nx=NX)
    hbs = [TB(A.bf16([D])) for _ in range(5)]
    mt = TB(A.bf16([4, 5, 128]))
    wp = TB(A.bf16([4, 2, 256]))
    scale_bc = TB(A.f32([D]))
    yT = [TB(A.bf16([8, 128])) for _ in range(2)]
    tmp = [TB(A.f32([D])) for _ in range(2)]
    s_c = kb.sem("pc")
    s_st = [kb.sem("pst") for _ in range(NX)]
    kb.dma("sp", mt.ap, I["c_pool"], inc=s_c)
    kb.dma("sp", scale_bc.ap, I["pool_scale"][j].partition_broadcast(128), inc=s_c)
    for gg in range(4):
        r_c = kb.dma("sp", wp.ap[:, gg, :, :], c.Wb["pool_w_group"][j][gg].rearrange("(c p) e -> p c e", p=128),
                     waits=[c.wready[c.pi]], inc=s_c)
    for tb_ in (mt, wp, scale_bc):
        tb_.w = r_c
    yps = [TB(c.ps[:, 0:1024].rearrange("p (a b) -> p a b", a=8, b=128)),
           TB(c.ps[:, 1024:2048].rearrange("p (a b) -> p a b", a=8, b=128))]
    zps = [TB(c.ps[:, 2048:3072]), TB(c.ps[:, 3072:4096])]

    def sl(b):
        xb = g["x"][b % NX]
        kb.xdma("sp", xb.ap, xd[b * 128:(b + 1) * 128, :], g["xl"][b % NX], writes=[xb])

    def s0(b):
        norm_ops(c, kb, g, b, g["x"][b % NX], hbs[b % 5])

    def sy(b):
        yp = yps[b % 2]
        sbs = [sb for sb in (b - 1, b, b + 1) if 0 <= sb < NB]
        fns = []
        for dc in range(8):
            gg = dc // 2
            for sb in sbs:
                if sb == b - 1:
                    kind = 0
                elif sb == b + 1:
                    kind = 2
                else:
                    kind = 3 if b == 0 else (4 if b == NB - 1 else 1)
                hb = hbs[sb % 5]
                fns.append(lambda e, dc=dc, gg=gg, kind=kind, hb=hb, first=(sb == sbs[0]), last=(sb == sbs[-1]): e.matmul(
                    yp.ap[:, dc, :], hb.ap[:, dc * 128:(dc + 1) * 128], mt.ap[:, gg, kind, :], start=first, stop=last))
        kb.x("pe", fns, reads=[hbs[sb % 5] for sb in sbs] + [mt], writes=[yp])

    def syc(b):
        kb.x("act", lambda e: e.activation(out=yT[b % 2].ap, in_=yps[b % 2].ap, func=AF.Copy), reads=[yps[b % 2]],
             writes=[yT[b % 2]])

    def sz(b):
        zp, yt = zps[b % 2], yT[b % 2]
        kb.x("pe", [lambda e, dc=dc: e.matmul(zp.ap[:, (dc // 2) * 256:(dc // 2 + 1) * 256], yt.ap[:, dc, :],
                                              wp.ap[:, dc // 2, dc % 2, :], start=(dc % 2 == 0), stop=(dc % 2 == 1))
                    for dc in range(8)], reads=[yt, wp], writes=[zp])

    def sfin(b):
        zp, tm, xb = zps[b % 2], tmp[b % 2], g["x"][b % NX]
        kb.x("dve", lambda e: e.tensor_tensor(out=tm.ap, in0=zp.ap, in1=scale_bc.ap, op=ALU.mult),
             reads=[zp, scale_bc], writes=[tm])
        kb.x("dve", lambda e: e.tensor_tensor(out=xb.ap, in0=tm.ap, in1=xb.ap, op=ALU.add), reads=[tm], writes=[xb])
        kb.xdma("sp", xd[b * 128:(b + 1) * 128, :], xb.ap, s_st[b % NX], reads=[xb])

    for t in range(NB + 7):
        for fn, lag in ((sfin, 6), (sz, 5), (syc, 4), (sy, 3), (s0, 1), (sl, 0)):
            i = t - lag
            if 0 <= i < NB:
                fn(i)
    kb.wait_only("sp", [(sm, sm.v) for sm in s_st])


def pool_phase(c, xd, layer):
    kb, A, I = c.kb, c.A, c.I
    j = layer // 2
    A.reset()
    NB = S // 128
    ns = NormStage(c, I["mix_norm"][layer], nx=5)
    hring = Ring([A.bf16([D]) for _ in range(5)])
    mt = A.bf16([4, 5, 128])
    wp = A.bf16([4, 2, 256])
    scale_bc = A.f32([D])
    yT = [A.bf16([8, 128]) for _ in range(2)]
    tmp = [A.f32([D]) for _ in range(2)]
    s_c = kb.sem("pc")
    kb.dma("sp", mt, I["c_pool"], inc=s_c)
    kb.dma("sp", scale_bc, I["pool_scale"][j].partition_broadcast(128), inc=s_c)
    for g in range(4):
        kb.dma("sp", wp[:, g, :, :], c.Wb["pool_w_group"][j][g].rearrange("(c p) e -> p c e", p=128),
               waits=[c.wready[c.pi]], inc=s_c)
    c_ready = (s_c, s_c.v)
    s_y, s_yc, s_z, s_t, s_o = (kb.sem(n) for n in ("py", "pyc", "pz", "pt", "po"))
    s_store = [kb.sem("pst") for _ in range(5)]
    yps = [c.ps[:, 0:1024].rearrange("p (a b) -> p a b", a=8, b=128),
           c.ps[:, 1024:2048].rearrange("p (a b) -> p a b", a=8, b=128)]
    zps = c.ps[:, 2048:3072]

    xitems = {}
    hitems = {}

    def do_norm(b):
        xitems[b] = ns.load(xd[b * 128:(b + 1) * 128, :])
        hi, hslot, hb, w = hring.acquire()
        r_h = ns.norm(xitems[b], hb, extra_waits=w)
        hitems[b] = (hi, hb, r_h)

    ystate = {}

    def do_y(b, u):
        yp = yps[u % 2]
        sbs = [sb for sb in (b - 1, b, b + 1) if 0 <= sb < NB]
        n_mm = 8 * len(sbs)
        k = 0
        for dc in range(8):
            g = dc // 2
            for sb in sbs:
                if sb == b - 1:
                    kind = 0
                elif sb == b + 1:
                    kind = 2
                else:
                    kind = 3 if b == 0 else (4 if b == NB - 1 else 1)
                hb = hitems[sb][1]
                k += 1
                r_y = kb.op("pe", lambda e, dc=dc, g=g, kind=kind, hb=hb, sb=sb: e.matmul(
                    yp[:, dc, :], hb[:, dc * 128:(dc + 1) * 128], mt[:, g, kind, :],
                    start=(sb == sbs[0]), stop=(sb == sbs[-1])),
                    waits=[hitems[x][2] for x in sbs] + [c_ready, (s_yc, u - 1)] if k == 1 else [],
                    inc=s_y if k == n_mm else None)
        if b - 1 >= 0:
            hring.release(hitems[b - 1][0], [r_y])
        if b == NB - 1:
            hring.release(hitems[b][0], [r_y])
        yt = yT[u % 2]
        r_yc = kb.op("act", lambda e: e.activation(out=yt, in_=yp, func=AF.Copy), waits=[r_y, (s_z, u - 1)], inc=s_yc)
        ystate[b] = (yt, r_yc)

    def do_z(b, u):
        yt, r_yc = ystate.pop(b)
        for dc in range(8):
            g, dcl = dc // 2, dc % 2
            r_z = kb.op("pe", lambda e, dc=dc, g=g, dcl=dcl: e.matmul(
                zps[:, g * 256:(g + 1) * 256], yt[:, dc, :], wp[:, g, dcl, :], start=(dcl == 0), stop=(dcl == 1)),
                waits=[r_yc, (s_o, u)] if dc == 0 else [], inc=s_z if dc == 7 else None)
        tm = tmp[u % 2]
        xi, xb, xr = xitems[b]
        r_t = kb.op("dve", lambda e: e.tensor_tensor(out=tm, in0=zps, in1=scale_bc, op=ALU.mult),
                    waits=[r_z, c_ready, (s_o, u - 1)], inc=s_t)
        r_o = kb.op("dve", lambda e: e.tensor_tensor(out=xb, in0=tm, in1=xb, op=ALU.add), waits=[r_t], inc=s_o)
        slot = xi % 5
        r_st = kb.dma("sp", xd[b * 128:(b + 1) * 128, :], xb, waits=[r_o], inc=s_store[slot])
        ns.xring.release(xi, [r_st])

    do_norm(0)
    do_norm(1)
    for b in range(NB):
        if b + 2 < NB:
            do_norm(b + 2)
        do_y(b, b)
        if b >= 1:
            do_z(b - 1, b - 1)
    do_z(NB - 1, NB - 1)
    kb.wait_only("sp", [(sm, sm.v) for sm in s_store])


def even_phase(c, xd, layer):
    c.A.reset()
    e1a_pipelined(c, xd, layer)
    barrier(c)
    if DEBUG_STOP in ("e1a", "e1a0", "e1a1"):
        return
    even_attn(c, xd, layer, None)

def norm_ops(c, kb, g, i, xb, hb):
    si = i % 3
    ss, mse, rstd, junk = g["ss"][si], g["mse"][si], g["rstd"][si], g["junk"]
    kb.x("act", lambda e: e.activation(out=junk.ap, in_=xb.ap, func=AF.Square, scale=1.0 / 32.0, accum_out=ss.ap),
         reads=[xb], writes=[ss, junk])
    kb.x("pool", lambda e: e.tensor_tensor(out=mse.ap, in0=ss.ap, in1=c.cst[:, 1:2], op=ALU.add),
         reads=[ss], writes=[mse], waits=[c.cst_ready])
    kb.x("pool", lambda e: e.tensor_tensor(out=rstd.ap, in0=mse.ap, in1=c.cst[:, 0:1], op=ALU.pow),
         reads=[mse], writes=[rstd])
    kb.x("dve", lambda e: e.scalar_tensor_tensor(out=hb.ap, in0=xb.ap, scalar=rstd.ap, in1=g["gain"].ap,
                                                 op0=ALU.mult, op1=ALU.mult),
         reads=[xb, rstd, g["gain"]], writes=[hb])


def norm_setup(c, kb, A, gain_ap, nx=3):
    g = {}
    g["gain"] = TB(A.f32([D]))
    g["ss"] = [TB(A.f32([1])) for _ in range(3)]
    g["mse"] = [TB(A.f32([1])) for _ in range(3)]
    g["rstd"] = [TB(A.f32([1])) for _ in range(3)]
    g["junk"] = TB(A.bf16([D]))
    g["x"] = [TB(A.f32([D])) for _ in range(nx)]
    g["xl"] = [kb.sem("xl") for _ in range(nx)]
    g["sg"] = kb.sem("gn")
    kb.xdma("sp", g["gain"].ap, gain_ap.partition_broadcast(128), g["sg"], writes=[g["gain"]])
    return g


def e1b_pipelined(c, xd, layer, qz, kT, PAD):
    kb, A, I = c.kb, c.A, c.I
    j = layer // 2
    NB = S // 128
    kb.eng_sems()
    win_src = c.Wb["even_w_in"][j].rearrange("(c p) f -> p c f", p=128)
    g = norm_setup(c, kb, A, I["mix_norm"][layer], nx=3)
    hb = [TB(A.bf16([D])) for _ in range(2)]
    hT = [TB(A.bf16([8, 128])) for _ in range(2)]
    win = TB(A.bf16([8, 1536]))
    gq = TB(A.f32([2, 2, 64]))
    cs = [TB(A.f32([2, 64])) for _ in range(3)]
    cg = [[TB(A.f32([2, 64])) for _ in range(4)] for _ in range(2)]
    xf = [[TB(A.f32([8, 64])) for _ in range(2)] for _ in range(2)]
    sq = [[TB(A.f32([8, 64])) for _ in range(2)] for _ in range(2)]
    ssq, mseq, rsq = ([TB(A.f32([8])) for _ in range(2)] for _ in range(3))
    t1 = [TB(A.f32([8, 64])) for _ in range(2)]
    t2 = [TB(A.f32([8, 64])) for _ in range(2)]
    qkb = [[TB(A.bf16([8, 64])) for _ in range(2)] for _ in range(2)]
    vs = [TB(A.bf16([8, VH])) for _ in range(2)]
    s_c = kb.sem("e2c")
    s_cs = [kb.sem("cs") for _ in range(3)]
    s_vst = [kb.sem("vst") for _ in range(2)]
    for k3 in range(3):
        kb.xdma("sp", win.ap[:, :, k3 * 512:(k3 + 1) * 512], win_src[:, :, 1024 + k3 * 512:1536 + k3 * 512], s_c,
                writes=[win] if k3 == 2 else [], waits=[c.wready[c.pi]])
    for wq_, nm_ in enumerate(("attn_q_norm", "attn_k_norm")):
        kb.dma("sp", gq.ap[:, wq_, 0, :], I[nm_][j].partition_broadcast(128), inc=s_c)
        kb.dma("sp", gq.ap[:, wq_, 1, 0:32], I[nm_][j][32:64].partition_broadcast(128), inc=s_c)
        kb.xdma("sp", gq.ap[:, wq_, 1, 32:64], I[nm_][j][0:32].partition_broadcast(128), s_c, writes=[gq])
    win.w = gq.w
    kb.x("pool", lambda e: e.memset(qz[0][64:128, :, :], 0.0))
    kb.x("pool", lambda e: e.memset(qz[1][0:64, :, :], 0.0))
    kb.x("pool", lambda e: e.memset(kT[:, :, 0:PAD], 0.0))
    kb.x("pool", lambda e: e.memset(kT[:, :, PAD + S:PAD + S + PAD], 0.0))
    for v_ in vs:
        kb.x("pool", lambda e, v_=v_: e.memset(v_.ap, 0.0), writes=[v_])
        kb.x("pool", lambda e, v_=v_: e.memset(v_.ap[:, :, 64:65], 1.0), writes=[v_])
    r_init = kb.x("pool", lambda e: e.memset(c.tiny[:, 6:7], 0.0))
    tp = TB(c.bank(0).bitcast(BF16)[:, 0:1024].rearrange("p (a b) -> p a b", a=8, b=128))
    zq = [TB(c.bank(1)), TB(c.bank(2))]
    zv = TB(c.bank(3))
    tqs = [[TB(c.bank(4 + 2 * par + w_).bitcast(BF16)[:, 0:512].rearrange("p (a b) -> p a b", a=4, b=128))
            for w_ in range(2)] for par in range(2)]

    def sl(b):
        nx = len(g["x"])
        xb = g["x"][b % nx]
        kb.xdma("sp", xb.ap, xd[b * 128:(b + 1) * 128, :], g["xl"][b % nx], writes=[xb])
        csb = cs[b % 3]
        kb.dma("sp", csb.ap[:, 0, :], I["c_cos"][b * 128:(b + 1) * 128, :],
               waits=[csb.w] + list(csb.rd.values()), inc=s_cs[b % 3])
        kb.xdma("sp", csb.ap[:, 1, :], I["c_sin"][b * 128:(b + 1) * 128, :], s_cs[b % 3], writes=[csb])

    def s0a(b):
        nx = len(g["x"])
        xb = g["x"][b % nx]
        norm_ops(c, kb, g, b, xb, hb[b % 2])
        csb = cs[b % 3]
        for which in range(2):
            cgb = cg[which][b % 4]
            kb.x("pool", lambda e, cgb=cgb, which=which: e.tensor_tensor(out=cgb.ap, in0=csb.ap, in1=gq.ap[:, which, :, :],
                                                                          op=ALU.mult), reads=[csb, gq], writes=[cgb])

    def s0b(b):
        h = hb[b % 2]
        kb.x("pe", [lambda e, dc=dc: e.transpose(out=tp.ap[:, dc, :], in_=h.ap[:, dc * 128:(dc + 1) * 128], identity=c.ident)
                    for dc in range(8)], reads=[h], writes=[tp], waits=[c.const_ready])
        kb.x("act", lambda e: e.activation(out=hT[b % 2].ap, in_=tp.ap, func=AF.Copy), reads=[tp], writes=[hT[b % 2]])

    def s1(b):
        hTb = hT[b % 2]
        kb.x("pe", [lambda e, dc=dc: e.matmul(zv.ap, hTb.ap[:, dc, :], win.ap[:, dc, 1024:1536], start=(dc == 0), stop=(dc == 7))
                    for dc in range(8)], reads=[hTb, win], writes=[zv])
        vsb = vs[b % 2]
        kb.x("act", lambda e: e.activation(out=vsb.ap[:, :, 0:64], in_=zv.ap.rearrange("p (h d) -> p h d", h=8, d=64),
                                           func=AF.Copy), reads=[zv], writes=[vsb])
        kb.xdma("sp", c.vaug_d[PAD + b * 128:PAD + (b + 1) * 128, :], vsb.ap.rearrange("p h d -> p (h d)"),
                s_vst[b % 2], reads=[vsb])
        for which in range(2):
            zp = zq[which]
            co = which * 512
            kb.x("pe", [lambda e, dc=dc, zp=zp, co=co: e.matmul(zp.ap, hTb.ap[:, dc, :], win.ap[:, dc, co:co + 512],
                                                                start=(dc == 0), stop=(dc == 7)) for dc in range(8)],
                 reads=[hTb, win], writes=[zp])
            zp3 = zp.ap.rearrange("p (h d) -> p h d", h=8, d=64)
            xfb, sqb = xf[which][b % 2], sq[which][b % 2]
            kb.x("act", lambda e, zp3=zp3, xfb=xfb: e.activation(out=xfb.ap, in_=zp3, func=AF.Copy), reads=[zp], writes=[xfb])
            kb.x("act", lambda e, zp3=zp3, sqb=sqb: e.activation(out=sqb.ap, in_=zp3, func=AF.Square), reads=[zp], writes=[sqb])

    def s2(b):
        for which in range(2):
            sqb = sq[which][b % 2]
            ssq_, mse_, rsq_ = ssq[which], mseq[which], rsq[which]
            kb.x("dve", lambda e, sqb=sqb, ssq_=ssq_: e.tensor_reduce(out=ssq_.ap, in_=sqb.ap, axis=AX.X, op=ALU.add),
                 reads=[sqb], writes=[ssq_])
            kb.x("dve", lambda e, ssq_=ssq_, mse_=mse_: e.tensor_scalar(out=mse_.ap, in0=ssq_.ap, scalar1=1.0 / 64.0,
                                                                       scalar2=EPS, op0=ALU.mult, op1=ALU.add),
                 reads=[ssq_], writes=[mse_])
            kb.x("pool", lambda e, mse_=mse_, rsq_=rsq_: e.tensor_tensor(out=rsq_.ap, in0=mse_.ap, in1=c.nh[:, 0:8], op=ALU.pow),
                 reads=[mse_], writes=[rsq_], waits=[c.cst_ready])
        for which in range(2):
            xfb, qb, cgb = xf[which][b % 2], qkb[which][b % 2], cg[which][b % 4]
            rsq_, t1_, t2_ = rsq[which], t1[which], t2[which]
            kb.x("dve", lambda e, xfb=xfb, rsq_=rsq_: e.tensor_tensor(out=xfb.ap, in0=xfb.ap,
                                                                     in1=rsq_.ap.unsqueeze(2).to_broadcast([128, 8, 64]),
                                                                     op=ALU.mult), reads=[rsq_], writes=[xfb])
            cb = cgb.ap[:, 0, :].unsqueeze(1).to_broadcast([128, 8, 64])
            sb0 = cgb.ap[:, 1, 0:32].unsqueeze(1).to_broadcast([128, 8, 32])
            sb1 = cgb.ap[:, 1, 32:64].unsqueeze(1).to_broadcast([128, 8, 32])
            kb.x("dve", lambda e, xfb=xfb, cb=cb, t1_=t1_: e.tensor_tensor(out=t1_.ap, in0=xfb.ap, in1=cb, op=ALU.mult),
                 reads=[xfb, cgb], writes=[t1_])
            kb.x("dve", [lambda e, xfb=xfb, sb0=sb0, t2_=t2_: e.tensor_tensor(out=t2_.ap[:, :, 0:32], in0=xfb.ap[:, :, 32:64],
                                                                             in1=sb0, op=ALU.mult),
                         lambda e, xfb=xfb, sb1=sb1, t2_=t2_: e.tensor_tensor(out=t2_.ap[:, :, 32:64], in0=xfb.ap[:, :, 0:32],
                                                                             in1=sb1, op=ALU.mult)],
                 reads=[xfb, cgb], writes=[t2_])
            kb.x("dve", lambda e, qb=qb, t1_=t1_, t2_=t2_: e.tensor_tensor(out=qb.ap, in0=t1_.ap, in1=t2_.ap, op=ALU.add),
                 reads=[t1_, t2_], writes=[qb])

    def s3a(b):
        for which in range(2):
            qb = qkb[which][b % 2]
            tq = tqs[b % 2][which]
            qbf = qb.ap.rearrange("p h d -> p (h d)")
            kb.x("pe", [lambda e, hp=hp, qbf=qbf, tq=tq: e.transpose(out=tq.ap[:, hp, :], in_=qbf[:, hp * 128:(hp + 1) * 128],
                                                                     identity=c.ident) for hp in range(4)],
                 reads=[qb], writes=[tq])

    def s3b(b):
        tq = tqs[b % 2][0]
        kb.x("act", [lambda e: e.activation(out=qz[0][0:64, :, b * 128:(b + 1) * 128], in_=tq.ap[0:64, :, :], func=AF.Copy),
                     lambda e: e.activation(out=qz[1][64:128, :, b * 128:(b + 1) * 128], in_=tq.ap[64:128, :, :],
                                            func=AF.Copy)], reads=[tq], waits=[r_init])
        tk = tqs[b % 2][1]
        kb.x("act", lambda e: e.activation(out=kT[:, :, PAD + b * 128:PAD + (b + 1) * 128], in_=tk.ap, func=AF.Copy),
             reads=[tk], waits=[r_init])

    pipeline(NB, [sl, s0a, s0b, s1, s2, s3a, s3b])
    kb.wait_only("sp", [(sm, sm.v) for sm in s_vst])


def e1b_old(c, xd, layer, qz, kT, PAD, win_src):
    kb, A, I = c.kb, c.A, c.I
    j = layer // 2
    NB = S // 128
    A.reset()
    ns = NormStage(c, I["mix_norm"][layer], nx=2)
    hblk = Ring([A.bf16([D]) for _ in range(2)])
    hT = [A.bf16([8, 128]) for _ in range(2)]
    win = A.bf16([8, 1536])
    gq_bc = A.f32([2, 64])
    cs = [A.f32([2, 64]) for _ in range(2)]
    xf = A.f32([8, 64])
    sq = A.f32([8, 64])
    ssq = A.f32([8])
    mseq = A.f32([8])
    rsq = A.f32([8])
    xn = xf
    xg = xf
    t1 = A.f32([8, 64])
    t2 = A.f32([8, 64])
    qkb = [A.bf16([8, 64]) for _ in range(2)]
    vs = [A.bf16([8, VH]) for _ in range(2)]
    s_c = kb.sem("e2c")
    kb.dma("sp", win[:, :, 0:512], win_src[:, :, 1024:1536], inc=s_c)
    kb.dma("sp", win[:, :, 512:1024], win_src[:, :, 1536:2048], inc=s_c)
    kb.dma("sp", win[:, :, 1024:1536], win_src[:, :, 2048:2560], inc=s_c)
    kb.dma("sp", gq_bc[:, 0, :], I["attn_q_norm"][j].partition_broadcast(128), inc=s_c)
    kb.dma("sp", gq_bc[:, 1, :], I["attn_k_norm"][j].partition_broadcast(128), inc=s_c)
    c_ready = (s_c, s_c.v)
    s_i = kb.sem("e2i")
    kb.op("pool", lambda e: e.memset(qz[0][64:128, :, :], 0.0), inc=s_i)
    kb.op("pool", lambda e: e.memset(qz[1][0:64, :, :], 0.0), inc=s_i)
    kb.op("pool", lambda e: e.memset(kT[:, :, 0:PAD], 0.0), inc=s_i)
    kb.op("pool", lambda e: e.memset(kT[:, :, PAD + S:PAD + S + PAD], 0.0), inc=s_i)
    for v_ in vs:
        kb.op("pool", lambda e, v_=v_: e.memset(v_, 0.0), inc=s_i)
    for v_ in vs:
        kb.op("pool", lambda e, v_=v_: e.memset(v_[:, :, 64:65], 1.0), waits=[(s_i, 6)], inc=s_i)
    i_ready = (s_i, s_i.v)
    tp = c.bank(0).bitcast(BF16)[:, 0:1024].rearrange("p (a b) -> p a b", a=8, b=128)
    zps = [c.bank(1), c.bank(2)]
    vps = c.bank(3)
    tq = c.bank(4).bitcast(BF16)[:, 0:512].rearrange("p (a b) -> p a b", a=4, b=128)
    (s_tp, s_tpc, s_z, s_xf, s_sq, s_ss, s_ms, s_rs, s_xn, s_xg, s_t1, s_t2, s_qb, s_tq, s_tqc, s_vz, s_vc) = (
        kb.sem(n) for n in ("tp", "tpc", "z", "xf", "sq", "ss", "ms", "rs", "xn", "xg", "t1", "t2", "qb", "tq", "tqc",
                            "vz", "vc"))
    s_cs = [kb.sem("cs") for _ in range(3)]
    s_vst = [kb.sem("vst") for _ in range(2)]
    def qkv_block(b, nq0):
        nq = nq0
        hb_i = b % 2
        xi = ns.load(xd[b * 128:(b + 1) * 128, :])
        hi, hslot, hb, w = hblk.acquire()
        r_h = ns.norm(xi, hb, extra_waits=w)
        ns.xring.release(xi[0], [r_h])
        for dc in range(8):
            r_tp = kb.op("pe", lambda e, dc=dc, hb=hb: e.transpose(out=tp[:, dc, :], in_=hb[:, dc * 128:(dc + 1) * 128],
                                                                   identity=c.ident),
                         waits=[r_h, (s_tpc, b), c.const_ready] if dc == 0 else [], inc=s_tp if dc == 7 else None)
        hblk.release(hi, [r_tp])
        hTb = hT[hb_i]
        r_hT = kb.op("act", lambda e, hTb=hTb: e.activation(out=hTb, in_=tp, func=AF.Copy), waits=[r_tp], inc=s_tpc)
        csb = cs[b % 2]
        wcs = [(s_t2, 4 * (b - 1))] if b >= 2 else []
        kb.dma("sp", csb[:, 0, :], I["c_cos"][b * 128:(b + 1) * 128, :], waits=wcs, inc=s_cs[b % 2])
        r_cs = kb.dma("sp", csb[:, 1, :], I["c_sin"][b * 128:(b + 1) * 128, :], inc=s_cs[b % 2])
        for dc in range(8):
            r_vz = kb.op("pe", lambda e, dc=dc, hTb=hTb: e.matmul(vps, hTb[:, dc, :], win[:, dc, 1024:1536],
                                                                  start=(dc == 0), stop=(dc == 7)),
                         waits=[r_hT, c_ready, (s_vc, b)] if dc == 0 else [], inc=s_vz if dc == 7 else None)
        vsb = vs[b % 2]
        wv = [(s_vst[b % 2], 16 * (b // 2))] if b >= 2 else []
        r_vc = kb.op("act", lambda e, vsb=vsb: e.activation(out=vsb[:, :, 0:64],
                                                            in_=vps.rearrange("p (h d) -> p h d", h=8, d=64), func=AF.Copy),
                     waits=[r_vz, i_ready] + wv, inc=s_vc)
        kb.dma("sp", c.vaug_d[PAD + b * 128:PAD + (b + 1) * 128, :], vsb.rearrange("p h d -> p (h d)"),
               waits=[r_vc], inc=s_vst[b % 2])
        for which in range(2):
            qk_unit(b, which, nq, hTb, csb, r_cs)
            nq += 1

    def qk_unit(b, which, nq, hTb, csb, r_cs):
        if True:
            zp = zps[which]
            co = which * 512
            for dc in range(8):
                r_z = kb.op("pe", lambda e, dc=dc, zp=zp, co=co, hTb=hTb: e.matmul(
                    zp, hTb[:, dc, :], win[:, dc, co:co + 512], start=(dc == 0), stop=(dc == 7)),
                    waits=[(s_sq, nq - 1)] if dc == 0 else [], inc=s_z if dc == 7 else None)
            zp3 = zp.rearrange("p (h d) -> p h d", h=8, d=64)
            r_xf = kb.op("act", lambda e, zp3=zp3: e.activation(out=xf, in_=zp3, func=AF.Copy),
                         waits=[r_z, (s_xn, nq), (s_xg, nq), (s_t1, nq), (s_t2, 2 * nq)], inc=s_xf)
            r_sq = kb.op("act", lambda e, zp3=zp3: e.activation(out=sq, in_=zp3, func=AF.Square),
                         waits=[r_z, (s_ss, nq)], inc=s_sq)
            r_ss = kb.op("dve", lambda e: e.tensor_reduce(out=ssq, in_=sq, axis=AX.X, op=ALU.add),
                         waits=[r_sq, (s_ms, nq)], inc=s_ss)
            r_ms = kb.op("dve", lambda e: e.tensor_scalar(out=mseq, in0=ssq, scalar1=1.0 / 64.0, scalar2=EPS,
                                                          op0=ALU.mult, op1=ALU.add),
                         waits=[r_ss, (s_rs, nq)], inc=s_ms)
            r_rs = kb.op("pool", lambda e: e.tensor_tensor(out=rsq, in0=mseq, in1=c.nh[:, 0:8], op=ALU.pow),
                         waits=[r_ms, c.cst_ready, (s_xn, nq)], inc=s_rs)
            r_xn = kb.op("dve", lambda e: e.tensor_tensor(out=xn, in0=xf, in1=rsq.unsqueeze(2).to_broadcast([128, 8, 64]),
                                                          op=ALU.mult),
                         waits=[r_rs, r_xf, (s_xg, nq)], inc=s_xn)
            gb = gq_bc[:, which, :].unsqueeze(1).to_broadcast([128, 8, 64])
            r_xg = kb.op("dve", lambda e, gb=gb: e.tensor_tensor(out=xg, in0=xn, in1=gb, op=ALU.mult),
                         waits=[r_xn, c_ready, (s_t1, nq), (s_t2, 2 * nq)], inc=s_xg)
            cb = csb[:, 0, :].unsqueeze(1).to_broadcast([128, 8, 64])
            r_t1 = kb.op("pool", lambda e, cb=cb: e.tensor_tensor(out=t1, in0=xg, in1=cb, op=ALU.mult),
                         waits=[r_xg, r_cs, (s_qb, nq)], inc=s_t1)
            sb0 = csb[:, 1, 0:32].unsqueeze(1).to_broadcast([128, 8, 32])
            sb1 = csb[:, 1, 32:64].unsqueeze(1).to_broadcast([128, 8, 32])
            kb.op("pool", lambda e, sb0=sb0: e.tensor_tensor(out=t2[:, :, 0:32], in0=xg[:, :, 32:64], in1=sb0, op=ALU.mult),
                  waits=[r_xg, (s_qb, nq)], inc=s_t2)
            r_t2 = kb.op("pool", lambda e, sb1=sb1: e.tensor_tensor(out=t2[:, :, 32:64], in0=xg[:, :, 0:32], in1=sb1,
                                                                     op=ALU.mult), inc=s_t2)
            qb = qkb[nq % 2]
            r_qb = kb.op("dve", lambda e, qb=qb: e.tensor_tensor(out=qb, in0=t1, in1=t2, op=ALU.add),
                         waits=[r_t1, r_t2, (s_tq, nq - 1)], inc=s_qb)
            qbf = qb.rearrange("p h d -> p (h d)")
            for hp in range(4):
                r_tq = kb.op("pe", lambda e, hp=hp, qbf=qbf: e.transpose(out=tq[:, hp, :], in_=qbf[:, hp * 128:(hp + 1) * 128],
                                                                         identity=c.ident),
                             waits=[r_qb, (s_tqc, nq)] if hp == 0 else [], inc=s_tq if hp == 3 else None)
            if which == 0:
                kb.op("act", lambda e: e.activation(out=qz[0][0:64, :, b * 128:(b + 1) * 128], in_=tq[0:64, :, :],
                                                    func=AF.Copy), waits=[r_tq, i_ready])
                kb.op("act", lambda e: e.activation(out=qz[1][64:128, :, b * 128:(b + 1) * 128], in_=tq[64:128, :, :],
                                                    func=AF.Copy), inc=s_tqc)
            else:
                dst = kT[:, :, PAD + b * 128:PAD + (b + 1) * 128]
                kb.op("act", lambda e, dst=dst: e.activation(out=dst, in_=tq, func=AF.Copy), waits=[r_tq, i_ready], inc=s_tqc)
    for b in range(NB):
        qkv_block(b, 2 * b)
    kb.wait_only("sp", [(sm, sm.v) for sm in s_vst])


def e1a_pipelined(c, xd, layer):
    kb, A, I = c.kb, c.A, c.I
    j = layer // 2
    NB = S // 128
    kb.eng_sems()
    win_src = c.Wb["even_w_in"][j].rearrange("(c p) f -> p c f", p=128)
    g = norm_setup(c, kb, A, I["mix_norm"][layer], nx=3)
    hb = [TB(A.bf16([D])) for _ in range(2)]
    hT = [TB(A.bf16([8, 128])) for _ in range(2)]
    win = TB(A.bf16([8, 1024]))
    ub = [TB(A.bf16([4, 128])) for _ in range(5)]
    gvf = [TB(A.f32([4, 128])) for _ in range(3)]
    sqv = [TB(A.f32([4, 128])) for _ in range(2)]
    gv = [TB(A.bf16([4, 128])) for _ in range(2)]
    ao = [TB(A.bf16([4, 128])) for _ in range(2)]
    aost = [TB(A.bf16([4, 128])) for _ in range(2)]
    ssv, msev = TB(A.f32([4])), TB(A.f32([4]))
    rsvs = [TB(A.f32([4])) for _ in range(2)]
    wsp_f = TB(A.f32([4, 128]))
    wsp_b = TB(A.bf16([4, 128]))
    wspT = TB(A.bf16([4, 128]))
    sgn = TB(A.f32([4, 128]))
    bspT = TB(A.f32([4]))
    s_c = kb.sem("e1c")
    s_ast = [kb.sem("ast") for _ in range(2)]
    kb.dma("sp", win.ap[:, :, 0:512], win_src[:, :, 0:512], waits=[c.wready[c.pi]], inc=s_c)
    kb.dma("sp", win.ap[:, :, 512:1024], win_src[:, :, 512:1024], inc=s_c)
    kb.dma("sp", wsp_f.ap, I["sgu_w_spatial"][j].rearrange("g t s -> t g s"), inc=s_c)
    kb.dma("sp", sgn.ap.rearrange("p g t -> p (g t)"),
           I["sgu_norm"][j].rearrange("g t -> (g t)").partition_broadcast(128), inc=s_c)
    r_c = kb.op("sp", lambda e: e.dma_start(out=bspT.ap, in_=I["sgu_b_spatial"][j].rearrange("g t -> t g"),
                                            allow_slow_non_contiguous=True), inc=s_c, n=16)
    for tb_ in (win, wsp_f, sgn, bspT):
        tb_.w = r_c
    tp = TB(c.bank(0).bitcast(BF16)[:, 0:1024].rearrange("p (a b) -> p a b", a=8, b=128))
    zu, zv = TB(c.bank(1)), TB(c.bank(2))
    mxs = [TB(c.bank(bk).rearrange("p (a b) -> p a b", a=4, b=128)) for bk in (3, 5)]
    tas = [TB(c.bank(bk).bitcast(BF16)[:, 0:512].rearrange("p (a b) -> p a b", a=4, b=128)) for bk in (4, 6)]
    tpw = TB(c.bank(7).bitcast(BF16)[:, 0:512].rearrange("p (a b) -> p a b", a=4, b=128))
    kb.x("dve", lambda e: e.tensor_copy(out=wsp_b.ap, in_=wsp_f.ap), reads=[wsp_f], writes=[wsp_b])
    kb.x("pe", [lambda e, gg=gg: e.transpose(out=tpw.ap[:, gg, :], in_=wsp_b.ap[:, gg, :], identity=c.ident) for gg in range(4)],
         reads=[wsp_b], writes=[tpw], waits=[c.const_ready])
    kb.x("act", lambda e: e.activation(out=wspT.ap, in_=tpw.ap, func=AF.Copy), reads=[tpw], writes=[wspT])
    aoT_dst = c.aoT_d.rearrange("(g d) t -> d g t", g=4, d=128)

    def sl(b):
        nx = len(g["x"])
        xb = g["x"][b % nx]
        kb.xdma("sp", xb.ap, xd[b * 128:(b + 1) * 128, :], g["xl"][b % nx], writes=[xb])

    def s0a(b):
        nx = len(g["x"])
        norm_ops(c, kb, g, b, g["x"][b % nx], hb[b % 2])

    def s0b(b):
        h = hb[b % 2]
        kb.x("pe", [lambda e, dc=dc: e.transpose(out=tp.ap[:, dc, :], in_=h.ap[:, dc * 128:(dc + 1) * 128], identity=c.ident)
                    for dc in range(8)], reads=[h], writes=[tp], waits=[c.const_ready])
        kb.x("act", lambda e: e.activation(out=hT[b % 2].ap, in_=tp.ap, func=AF.Copy), reads=[tp], writes=[hT[b % 2]])

    def s1(b):
        hTb = hT[b % 2]
        kb.x("pe", [lambda e, dc=dc: e.matmul(zu.ap, hTb.ap[:, dc, :], win.ap[:, dc, 0:512], start=(dc == 0), stop=(dc == 7))
                    for dc in range(8)], reads=[hTb, win], writes=[zu])
        kb.x("act", lambda e: e.activation(out=ub[b % 5].ap.rearrange("p g d -> p (g d)"), in_=zu.ap, func=AF.Gelu),
             reads=[zu], writes=[ub[b % 5]])
        kb.x("pe", [lambda e, dc=dc: e.matmul(zv.ap, hTb.ap[:, dc, :], win.ap[:, dc, 512:1024], start=(dc == 0), stop=(dc == 7))
                    for dc in range(8)], reads=[hTb, win], writes=[zv])
        gf, sv = gvf[b % 3], sqv[b % 2]
        kb.x("act", lambda e: e.activation(out=gf.ap.rearrange("p g d -> p (g d)"), in_=zv.ap, func=AF.Gelu),
             reads=[zv], writes=[gf])
        kb.x("act", lambda e: e.activation(out=sv.ap, in_=gf.ap, func=AF.Square), reads=[gf], writes=[sv])

    def s2a(b):
        sv, rsv = sqv[b % 2], rsvs[b % 2]
        kb.x("dve", lambda e: e.tensor_reduce(out=ssv.ap, in_=sv.ap, axis=AX.X, op=ALU.add), reads=[sv], writes=[ssv])
        kb.x("dve", lambda e: e.tensor_scalar(out=msev.ap, in0=ssv.ap, scalar1=1.0 / 128.0, scalar2=EPS,
                                              op0=ALU.mult, op1=ALU.add), reads=[ssv], writes=[msev])
        kb.x("pool", lambda e: e.tensor_tensor(out=rsv.ap, in0=msev.ap, in1=c.nh[:, 0:4], op=ALU.pow),
             reads=[msev], writes=[rsv], waits=[c.cst_ready])

    def s2b(b):
        gf, gvb, rsv = gvf[b % 3], gv[b % 2], rsvs[b % 2]
        kb.x("dve", [lambda e, gg=gg: e.scalar_tensor_tensor(out=gvb.ap[:, gg, :], in0=gf.ap[:, gg, :],
                                                             scalar=rsv.ap[:, gg:gg + 1], in1=sgn.ap[:, gg, :],
                                                             op0=ALU.mult, op1=ALU.mult) for gg in range(4)],
             reads=[gf, rsv, sgn], writes=[gvb])

    def s3a(b):
        gvb, mx = gv[b % 2], mxs[b % 2]
        kb.x("pe", [lambda e, gg=gg: e.matmul(mx.ap[:, gg, :], wspT.ap[:, gg, :], gvb.ap[:, gg, :], start=True, stop=True)
                    for gg in range(4)], reads=[gvb, wspT], writes=[mx])

    def s3b(b):
        aob, ubb, mx = ao[b % 2], ub[b % 5], mxs[b % 2]
        kb.x("dve", [lambda e, gg=gg: e.scalar_tensor_tensor(out=aob.ap[:, gg, :], in0=mx.ap[:, gg, :],
                                                             scalar=bspT.ap[:, gg:gg + 1], in1=ubb.ap[:, gg, :],
                                                             op0=ALU.add, op1=ALU.mult) for gg in range(4)],
             reads=[mx, bspT, ubb], writes=[aob])

    def s4a(b):
        aob, ta = ao[b % 2], tas[b % 2]
        kb.x("pe", [lambda e, gg=gg: e.transpose(out=ta.ap[:, gg, :], in_=aob.ap[:, gg, :], identity=c.ident)
                    for gg in range(4)], reads=[aob], writes=[ta])

    def s4b(b):
        ta, stb = tas[b % 2], aost[b % 2]
        kb.x("act", lambda e: e.activation(out=stb.ap, in_=ta.ap, func=AF.Copy), reads=[ta], writes=[stb])
        kb.xdma("sp", aoT_dst[:, :, b * 128:(b + 1) * 128], stb.ap, s_ast[b % 2], reads=[stb])

    pipeline(NB, [sl, s0a, s0b, s1, s2a, s2b, s3a, s3b, s4a, s4b])
    kb.wait_only("sp", [(sm, sm.v) for sm in s_ast])


def e3_pipelined(c, xd, layer):
    kb, A, I = c.kb, c.A, c.I
    j = layer // 2
    NB = S // 128
    kb.eng_sems()
    wout = TB(A.bf16([8, D]))
    o3 = [[TB(A.f32([8, OH])) for _ in range(3)] for _ in range(3)]
    s_ol = [kb.sem("ol") for _ in range(3)]
    xs = [TB(A.f32([D])) for _ in range(4)]
    s_xl = [kb.sem("xl") for _ in range(4)]
    s_xs = [kb.sem("xs") for _ in range(4)]
    at = [TB(A.bf16([4, 128])) for _ in range(4)]
    s_al = [kb.sem("al") for _ in range(4)]
    rec = TB(A.f32([8]))
    bo = [TB(A.bf16([8, 64])) for _ in range(2)]
    boT = [TB(A.bf16([4, 128])) for _ in range(2)]
    s_c = kb.sem("e3c")
    wsrc = c.Wb["even_w_out"][j].rearrange("(c p) f -> p c f", p=128)
    kb.dma("sp", wout.ap[:, 0:4, :], wsrc[:, 0:4, :], waits=[c.wready[c.pi]], inc=s_c)
    kb.xdma("sp", wout.ap[:, 4:8, :], wsrc[:, 4:8, :], s_c, writes=[wout])
    tb = TB(c.bank(0).bitcast(BF16)[:, 0:512].rearrange("p (a b) -> p a b", a=4, b=128))
    yps = [TB(c.ps[:, 512:1536]), TB(c.ps[:, 1536:2560])]
    aoT_src = c.aoT_d.rearrange("(g d) t -> d g t", g=4, d=128)

    def sl(b):
        ob3 = o3[b % 3]
        for bi in range(3):
            kb.xdma("sp", ob3[bi].ap.rearrange("p h d -> p (h d)"), c.o_d[bi][b * 128:(b + 1) * 128, :], s_ol[b % 3],
                    writes=[ob3[bi]])
        for bi in range(3):
            ob3[bi].w = ob3[2].w
        kb.xdma("sp", xs[b % 4].ap, xd[b * 128:(b + 1) * 128, :], s_xl[b % 4], writes=[xs[b % 4]])
        kb.xdma("sp", at[b % 4].ap, aoT_src[:, :, b * 128:(b + 1) * 128], s_al[b % 4], writes=[at[b % 4]])

    def s0(b):
        ob3 = o3[b % 3]
        o0, o1, o2 = ob3
        kb.x("dve", lambda e: e.tensor_tensor(out=o0.ap, in0=o0.ap, in1=o1.ap, op=ALU.add), reads=[o1], writes=[o0])
        kb.x("dve", lambda e: e.tensor_tensor(out=o0.ap, in0=o0.ap, in1=o2.ap, op=ALU.add), reads=[o2], writes=[o0])
        kb.x("dve", lambda e: e.reciprocal(out=rec.ap, in_=o0.ap[:, :, 64]), reads=[o0], writes=[rec])
        bb = bo[b % 2]
        kb.x("dve", lambda e: e.tensor_tensor(out=bb.ap, in0=o0.ap[:, :, 0:64],
                                              in1=rec.ap.unsqueeze(2).to_broadcast([128, 8, 64]), op=ALU.mult),
             reads=[o0, rec], writes=[bb])

    def s1(b):
        bb, bt = bo[b % 2], boT[b % 2]
        bbf = bb.ap.rearrange("p h d -> p (h d)")
        kb.x("pe", [lambda e, hp=hp: e.transpose(out=tb.ap[:, hp, :], in_=bbf[:, hp * 128:(hp + 1) * 128], identity=c.ident)
                    for hp in range(4)], reads=[bb], writes=[tb], waits=[c.const_ready])
        kb.x("act", lambda e: e.activation(out=bt.ap, in_=tb.ap, func=AF.Copy), reads=[tb], writes=[bt])

    def s2(b):
        bt, a_, yp, xb = boT[b % 2], at[b % 4], yps[b % 2], xs[b % 4]
        fns = []
        for dh in range(2):
            for mc in range(8):
                lhsT = a_.ap[:, mc, :] if mc < 4 else bt.ap[:, mc - 4, :]
                fns.append(lambda e, dh=dh, mc=mc, lhsT=lhsT: e.matmul(
                    yp.ap[:, dh * 512:(dh + 1) * 512], lhsT, wout.ap[:, mc, dh * 512:(dh + 1) * 512],
                    start=(mc == 0), stop=(mc == 7)))
        kb.x("pe", fns, reads=[bt, a_, wout], writes=[yp])
        kb.x("dve", [lambda e, dh=dh: e.tensor_tensor(out=xb.ap[:, dh * 512:(dh + 1) * 512],
                                                      in0=yp.ap[:, dh * 512:(dh + 1) * 512],
                                                      in1=xb.ap[:, dh * 512:(dh + 1) * 512], op=ALU.add) for dh in range(2)],
             reads=[yp], writes=[xb])
        kb.xdma("sp", xd[b * 128:(b + 1) * 128, :], xb.ap, s_xs[b % 4], reads=[xb])

    pipeline(NB, [sl, s0, s1, s2])
    kb.wait_only("sp", [(sm, sm.v) for sm in s_xs])


def even_attn(c, xd, layer, aoT):
    kb, A, I = c.kb, c.A, c.I
    j = layer // 2
    NB = S // 128
    PAD = 1024
    A.reset()
    A.push()
    qz = [A.bf16([4, S]), A.bf16([4, S])]
    kT = A.bf16([4, S + 2 * PAD])
    A.freeze()
    win_src = c.Wb["even_w_in"][j].rearrange("(c p) f -> p c f", p=128)

    A.reset()
    if not _os.environ.get("KOLD"):
        e1b_pipelined(c, xd, layer, qz, kT, PAD)
    else:
        e1b_old(c, xd, layer, qz, kT, PAD, win_src)
    barrier(c)
    if DEBUG_STOP == "e1b":
        A.pop()
        return

    c.cast_pump(16)
    A.reset()
    vring = Ring([A.bf16([8, VH]) for _ in range(7)])
    s_vl = [kb.sem("vl") for _ in range(7)]
    pT = [A.bf16([2, 2, 128]) for _ in range(5)]
    NPT = 5
    osb = [A.f32([8, OH]) for _ in range(2)]
    s_ost = [kb.sem("ost") for _ in range(2)]
    s_sc, s_ex, s_mk, s_pv, s_oc = (kb.sem(n) for n in ("sc", "ex", "mk", "pv", "oc"))
    sps = [c.bank(bk).rearrange("p (h a q) -> p h a q", h=2, a=2, q=128) for bk in (0, 1, 6, 7)]
    NSP = len(sps)
    ops_ = [(c.bank(2)[:, 0:4 * OH].rearrange("p (h d) -> p h d", h=4, d=OH),
             c.bank(3)[:, 0:4 * OH].rearrange("p (h d) -> p h d", h=4, d=OH)),
            (c.bank(4)[:, 0:4 * OH].rearrange("p (h d) -> p h d", h=4, d=OH),
             c.bank(5)[:, 0:4 * OH].rearrange("p (h d) -> p h d", h=4, d=OH))]
    mask4 = c.mask2.rearrange("p (h a q) -> p h a q", h=2, a=2, q=128)

    units = []
    for bi, dil in enumerate((1, 4, 16)):
        NJ = S // dil // 128
        for r in range(dil):
            for jq in range(NJ):
                for hp in range(4):
                    units.append((bi, dil, r, jq, hp, NJ))
    if _os.environ.get("KE2N"):
        units = units[:int(_os.environ["KE2N"])]
    vt = {}

    def load_v(bi, dil, r, t):
        vi, slot, vb, w = vring.acquire()
        row0 = PAD + r + dil * (128 * t - 64)
        src = c.vaug_d[row0:row0 + 128 * dil, :].rearrange("(i d) f -> i d f", d=dil)[:, 0, :]
        rr = kb.dma("sp", vb.rearrange("p h d -> p (h d)"), src, waits=w, inc=s_vl[slot])
        vt[(bi, r, t)] = (vi, vb, rr)

    HB = 0 if _os.environ.get("KE2") == "hh0" else 64

    def scores(u):
        bi, dil, r, jq, hp, NJ = units[u]
        sp = sps[u % NSP]
        q0 = r + dil * 128 * jq
        n = 0
        for hh in range(2):
            for ab in range(2):
                k0 = PAD + r + dil * (128 * (jq + ab) - 64)
                n += 1
                rr = kb.op("pe", lambda e, sp=sp, hh=hh, ab=ab, hp=hp, k0=k0, q0=q0, dil=dil: e.matmul(
                    sp[:, hh, ab, :], kT[:, hp, k0:k0 + 127 * dil + 1:dil],
                    qz[hh][:, hp, q0:q0 + 127 * dil + 1:dil], start=True, stop=True),
                    waits=[(s_ex, u - NSP + 1)] if n == 1 else [], inc=s_sc if n == 4 else None)
        return rr

    def softmax_pv(u, r_sc):
        bi, dil, r, jq, hp, NJ = units[u]
        sp = sps[u % NSP]
        pt = pT[u % NPT]
        r_ex = kb.op("act", lambda e, sp=sp, pt=pt: e.activation(out=pt, in_=sp, func=AF.Exp, scale=0.125),
                     waits=[r_sc, (s_pv, u - NPT + 1)], inc=s_ex)
        r_mk = kb.op("dve", lambda e, pt=pt: e.tensor_tensor(out=pt, in0=pt, in1=mask4, op=ALU.mult),
                     waits=[r_ex, c.const_ready], inc=s_mk)
        blk = u // 4
        olo, ohi = ops_[blk % 2]
        n = 0
        for hh in range(2):
            h = hp * 2 + hh
            ot = (olo if h < 4 else ohi)[:, h % 4, :]
            for ab in range(2):
                vi, vb, vr = vt[(bi, r, jq + ab)]
                n += 1
                rr = kb.op("pe", lambda e, ot=ot, pt=pt, hh=hh, ab=ab, vb=vb, h=h: e.matmul(
                    ot, pt[:, hh, ab, :], vb[:, h, 0:OH], start=(ab == 0), stop=(ab == 1)),
                    waits=[r_mk, vr, vt[(bi, r, jq + 1)][2], (s_oc, 2 * (blk - 1))] if n == 1 else [],
                    inc=s_pv if n == 4 else None)
        return rr

    pend = []
    LA = NSP - 1

    def finish(pu, pr):
        r_pv = softmax_pv(pu, pr)
        _attn_block_end(c, kb, units, pu, r_pv, vt, vring, ops_, osb, s_oc, s_ost)

    for u in range(len(units)):
        bi, dil, r, jq, hp, NJ = units[u]
        if hp == 0:
            if jq == 0:
                load_v(bi, dil, r, 0)
                load_v(bi, dil, r, 1)
            if jq + 2 <= NJ:
                load_v(bi, dil, r, jq + 2)
        pend.append((u, scores(u)))
        if len(pend) > LA:
            finish(*pend.pop(0))
    while pend:
        finish(*pend.pop(0))
    kb.wait_only("sp", [(sm, sm.v) for sm in s_ost])
    barrier(c)
    A.pop()
    if DEBUG_STOP == "e2":
        return
    c.A.reset()
    e3_pipelined(c, xd, layer)


def _attn_block_end(c, kb, units, u, r_pv, vt, vring, ops_, osb, s_oc, s_ost):
    bi, dil, r, jq, hp, NJ = units[u]
    if hp != 3:
        return
    blk = u // 4
    olo, ohi = ops_[blk % 2]
    ob = osb[blk % 2]
    wst = [(s_ost[blk % 2], 16 * (blk // 2))] if blk >= 2 else []
    kb.op("dve", lambda e: e.tensor_copy(out=ob[:, 0:4, :], in_=olo), waits=[r_pv] + wst, inc=s_oc)
    r_oc = kb.op("dve", lambda e: e.tensor_copy(out=ob[:, 4:8, :], in_=ohi), inc=s_oc)
    row0 = r + dil * 128 * jq
    dst = c.o_d[bi][row0:row0 + 128 * dil, :].rearrange("(i d) f -> i d f", d=dil)[:, 0, :]
    kb.dma("sp", dst, ob.rearrange("p h d -> p (h d)"), waits=[r_oc], inc=s_ost[blk % 2])
    vring.release(vt[(bi, r, jq)][0], [r_pv])
    if jq == NJ - 1:
        vring.release(vt[(bi, r, jq + 1)][0], [r_pv])


def even_out(c, xd, layer, aoT):
    kb, A, I = c.kb, c.A, c.I
    j = layer // 2
    NB = S // 128
    A.reset()
    wout = A.bf16([8, D])
    oring = Ring([[A.f32([8, OH]) for _ in range(3)] for _ in range(2)])
    s_ol = [kb.sem("ol") for _ in range(2)]
    xring = Ring([A.f32([D]) for _ in range(3)])
    s_xl = [kb.sem("xl") for _ in range(3)]
    s_xs = [kb.sem("xs") for _ in range(3)]
    rec = A.f32([8])
    bo = [A.bf16([8, 64]) for _ in range(2)]
    boT = [A.bf16([4, 128]) for _ in range(2)]
    s_c = kb.sem("e3c")
    wsrc = c.Wb["even_w_out"][j].rearrange("(c p) f -> p c f", p=128)
    kb.dma("sp", wout[:, 0:4, :], wsrc[:, 0:4, :], inc=s_c)
    kb.dma("sp", wout[:, 4:8, :], wsrc[:, 4:8, :], inc=s_c)
    c_ready = (s_c, s_c.v)
    s_a1, s_a2, s_rc, s_bo, s_tb, s_tbc, s_pj, s_ad = (kb.sem(n) for n in ("a1", "a2", "rc", "bo", "tb", "tbc", "pj", "ad"))
    tb = c.bank(0).bitcast(BF16)[:, 0:512].rearrange("p (a b) -> p a b", a=4, b=128)
    yps = [c.ps[:, 1024:2048], c.ps[:, 2048:3072]]

    def loads(b):
        oi, oslot, ob3, w = oring.acquire()
        for bi in range(3):
            r_o = kb.dma("sp", ob3[bi].rearrange("p h d -> p (h d)"), c.o_d[bi][b * 128:(b + 1) * 128, :],
                         waits=w if bi == 0 else [], inc=s_ol[oslot])
        xi, xslot, xb, wx = xring.acquire()
        r_x = kb.dma("sp", xb, xd[b * 128:(b + 1) * 128, :], waits=wx, inc=s_xl[xslot])
        return (oi, ob3, r_o, xi, xslot, xb, r_x)

    items = {0: loads(0)}
    for b in range(NB):
        if b + 1 < NB:
            items[b + 1] = loads(b + 1)
        oi, ob3, r_o, xi, xslot, xb, r_x = items[b]
        o0, o1, o2 = ob3
        r1 = kb.op("dve", lambda e, o0=o0, o1=o1: e.tensor_tensor(out=o0, in0=o0, in1=o1, op=ALU.add), waits=[r_o], inc=s_a1)
        r2 = kb.op("dve", lambda e, o0=o0, o2=o2: e.tensor_tensor(out=o0, in0=o0, in1=o2, op=ALU.add), waits=[r1], inc=s_a2)
        r3 = kb.op("dve", lambda e, o0=o0: e.reciprocal(out=rec, in_=o0[:, :, 64]), waits=[r2, (s_bo, b)], inc=s_rc)
        bb = bo[b % 2]
        r4 = kb.op("dve", lambda e, o0=o0, bb=bb: e.tensor_tensor(
            out=bb, in0=o0[:, :, 0:64], in1=rec.unsqueeze(2).to_broadcast([128, 8, 64]), op=ALU.mult),
            waits=[r3, (s_tb, b - 1)], inc=s_bo)
        oring.release(oi, [r4])
        bbf = bb.rearrange("p h d -> p (h d)")
        for hp in range(4):
            r5 = kb.op("pe", lambda e, hp=hp, bbf=bbf: e.transpose(out=tb[:, hp, :], in_=bbf[:, hp * 128:(hp + 1) * 128],
                                                                   identity=c.ident),
                       waits=[r4, c.const_ready, (s_tbc, b)] if hp == 0 else [], inc=s_tb if hp == 3 else None)
        bt = boT[b % 2]
        r6 = kb.op("act", lambda e, bt=bt: e.activation(out=bt, in_=tb, func=AF.Copy), waits=[r5, (s_pj, b - 1)], inc=s_tbc)
        yp = yps[b % 2]
        for dh in range(2):
            for mc in range(8):
                lhsT = aoT[:, mc, b * 128:(b + 1) * 128] if mc < 4 else bt[:, mc - 4, :]
                r7 = kb.op("pe", lambda e, dh=dh, mc=mc, lhsT=lhsT, yp=yp: e.matmul(
                    yp[:, dh * 512:(dh + 1) * 512], lhsT, wout[:, mc, dh * 512:(dh + 1) * 512],
                    start=(mc == 0), stop=(mc == 7)),
                    waits=[r6, c_ready, (s_ad, 2 * (b - 1))] if (dh == 0 and mc == 0) else [],
                    inc=s_pj if (dh == 1 and mc == 7) else None)
        for dh in range(2):
            xs = xb[:, dh * 512:(dh + 1) * 512]
            r8 = kb.op("dve", lambda e, xs=xs, dh=dh, yp=yp: e.tensor_tensor(out=xs, in0=yp[:, dh * 512:(dh + 1) * 512],
                                                                             in1=xs, op=ALU.add),
                       waits=[r7, r_x] if dh == 0 else [], inc=s_ad)
        r9 = kb.dma("sp", xd[b * 128:(b + 1) * 128, :], xb, waits=[r8], inc=s_xs[xslot])
        xring.release(xi, [r9])
    kb.wait_only("sp", [(sm, sm.v) for sm in s_xs])


_CACHE = {}


def _pool_consts():
    wins = (2, 4, 8, 16)
    out = np.zeros((128, 4, 5, 128), np.float32)
    for g, win in enumerate(wins):
        lo = win // 2
        hi = win - 1 - lo

        def m(t, s_):
            start = min(max(t - lo, 0), S)
            end = min(max(t + hi + 1, 0), S)
            v = 0.0
            if start <= s_ < end:
                v = 1.0 / float(end - start)
            if s_ == t:
                v -= 1.0
            return v
        for tl in range(128):
            for sl in range(128):
                out[sl, g, 0, tl] = m(512 + tl, 512 - 128 + sl)
                out[sl, g, 1, tl] = m(512 + tl, 512 + sl)
                out[sl, g, 2, tl] = m(512 + tl, 512 + 128 + sl)
                out[sl, g, 3, tl] = m(tl, sl)
                out[sl, g, 4, tl] = m(S - 128 + tl, S - 128 + sl)
    return out.astype(ml_dtypes.bfloat16)


def _consts():
    kk = np.arange(128)[:, None]
    qq = np.arange(128)[None, :]
    ma = (kk >= qq).astype(np.float32)
    mb = (kk <= qq).astype(np.float32)
    m2 = np.concatenate([ma, mb, ma, mb], axis=1)
    pos = np.arange(S, dtype=np.float32)
    inv_freq = (np.float32(10000.0) ** (-np.arange(0, 64, 2, dtype=np.float32) / np.float32(64))).astype(np.float32)
    ang = (pos[:, None] * inv_freq[None, :]).astype(np.float32)
    ang = np.concatenate([ang, ang], axis=-1)
    cos = np.cos(ang).astype(np.float32)
    sin = np.sin(ang).astype(np.float32)
    sin[:, :32] *= -1.0
    return {"c_ident": np.eye(128, dtype=np.float32).astype(ml_dtypes.bfloat16),
            "c_pool": _pool_consts(),
            "c_mask": m2.astype(ml_dtypes.bfloat16),
            "c_cos": np.ascontiguousarray(cos), "c_sin": np.ascontiguousarray(sin)}


def kernel(**inputs):
    nph = inputs.pop("_nphases", 99)
    skip = inputs.pop("_skip", 0)
    cores = inputs.pop("_cores", NCORES)
    trace = inputs.pop("_trace", False)
    key = (nph, skip)
    if key not in _CACHE:
        _CACHE[key] = build_program(nph, skip)
    nc = _CACHE[key]
    x = np.asarray(inputs["x"], dtype=np.float32)
    shared = {k: np.ascontiguousarray(np.asarray(v, dtype=np.float32)) for k, v in inputs.items() if k != "x"}
    if "consts" not in _CACHE:
        _CACHE["consts"] = _consts()
    shared.update(_CACHE["consts"])
    in_maps = []
    for ci in range(cores):
        m = dict(shared)
        m["x"] = np.ascontiguousarray(x[ci])
        in_maps.append(m)
    res = run_bass_kernel_spmd(nc, in_maps, core_ids=list(range(cores)), **({"trace": True} if trace else {}))
    if trace:
        print("EXEC_NS", res.exec_time_ns)
    return np.stack([np.asarray(r["out"], dtype=np.float32) for r in res.results], axis=0)
```

```python
import contextlib
import math
import numpy as np
import ml_dtypes
import concourse.bass as bass
import concourse.mybir as mybir
from concourse.bass_utils import run_bass_kernel_spmd

F32 = mybir.dt.float32
BF16 = mybir.dt.bfloat16
AF = mybir.ActivationFunctionType
ALU = mybir.AluOpType
AX = mybir.AxisListType

S = 4096
D = 1024
DFF = 2816
NFC = DFF // 128
DEPTH = 4
EPS = 1e-6
NCORES = 8
INW = 2560
VH = 80
VW = 8 * VH
OH = 66
OW = 8 * OH
import os as _os
DEBUG_STOP = _os.environ.get("KSTOP")


class Sem:
    def __init__(self, h):
        self.h = h
        self.v = 0


class Arena:
    def __init__(self, ap, nwords):
        self.ap = ap
        self.n = nwords
        self.off = 0
        self.base = 0

    def _take(self, words):
        o = self.off
        self.off += words
        assert self.off <= self.n, f"arena overflow {self.off} > {self.n}"
        self.hw = max(getattr(self, "hw", 0), self.off)
        return o

    def f32(self, shape):
        n = int(np.prod(shape))
        o = self._take(n)
        v = self.ap[:, o:o + n]
        return self._shape(v, shape)

    def bf16(self, shape):
        n = int(np.prod(shape))
        w = (n + 1) // 2
        o = self._take(w)
        v = self.ap[:, o:o + w].bitcast(BF16)
        if 2 * w != n:
            v = v[:, 0:n]
        return self._shape(v, shape)

    @staticmethod
    def _shape(v, shape):
        if len(shape) == 1:
            return v
        if len(shape) == 2:
            return v.rearrange("p (a b) -> p a b", a=shape[0], b=shape[1])
        if len(shape) == 3:
            return v.rearrange("p (a b c) -> p a b c", a=shape[0], b=shape[1], c=shape[2])
        raise ValueError

    def freeze(self):
        self.base = self.off

    def reset(self):
        self.off = self.base

    def push(self):
        self.stk = getattr(self, "stk", [])
        self.stk.append(self.base)
        self.base = self.off

    def pop(self):
        self.base = self.stk.pop()
        self.off = self.base


class Ring:
    def __init__(self, bufs):
        self.bufs = bufs
        self.n = len(bufs)
        self.i = 0
        self.free = []

    def acquire(self):
        i = self.i
        self.i += 1
        self.free.append(None)
        w = []
        if i >= self.n:
            w = self.free[i - self.n]
            assert w is not None, "ring slot reused before consumer recorded"
        return i, i % self.n, self.bufs[i % self.n], list(w)

    def release(self, i, waits):
        self.free[i] = [w for w in waits if w is not None]


class TB:
    def __init__(self, ap):
        self.ap = ap
        self.w = None
        self.rd = {}


def pipeline(nitems, stages):
    ns = len(stages)
    for t in range(nitems + ns - 1):
        for sidx in reversed(range(ns)):
            i = t - sidx
            if 0 <= i < nitems:
                stages[sidx](i)


class KB:
    ENG = ("pe", "act", "dve", "pool", "sp")

    def __init__(self, nc, stack):
        self.nc = nc
        self.stack = stack
        self.q = {e: [] for e in self.ENG}
        self.waited = {e: {} for e in self.ENG}
        self.nsem = 0
        self.pool = []
        self.phase_sems = []
        self.log = {e: [] for e in self.ENG}

    def check_deadlock(self):
        val = {}
        pc = {e: 0 for e in self.ENG}
        prog = True
        while prog:
            prog = False
            for e in self.ENG:
                while pc[e] < len(self.log[e]):
                    waits, inc, _ = self.log[e][pc[e]]
                    if all(val.get(sid, 0) >= v for sid, v in waits):
                        if inc is not None and inc[1] == "clear":
                            for sid in inc[0]:
                                val[sid] = 0
                        elif inc is not None:
                            val[inc[0]] = val.get(inc[0], 0) + inc[1]
                        pc[e] += 1
                        prog = True
                    else:
                        break
        stuck = {e: (pc[e], len(self.log[e])) for e in self.ENG if pc[e] < len(self.log[e])}
        return stuck, val

    def sem(self, name=None, persistent=False):
        if not persistent and self.pool:
            sm = self.pool.pop()
        else:
            self.nsem += 1
            h = self.stack.enter_context(self.nc.semaphore(f"s{self.nsem}_{name or ''}"))
            sm = Sem(h)
        if not persistent:
            self.phase_sems.append(sm)
        return sm

    def end_phase(self):
        if not _os.environ.get("KNOPOOL"):
            self.pool.extend(self.phase_sems)
        self.phase_sems = []

    def op(self, eng, fn, waits=(), inc=None, n=None):
        ws = []
        for w in waits:
            if w is None:
                continue
            s, v = w
            if v <= 0:
                continue
            if self.waited[eng].get(id(s), 0) >= v:
                continue
            self.waited[eng][id(s)] = v
            ws.append((s.h, v))
        amt = None
        if inc is not None:
            amt = n if n is not None else 1
            inc.v += amt
        h = inc.h if inc is not None else None

        def run(e, ws=ws, fn=fn, h=h, amt=amt):
            for sh, v in ws:
                e.wait_ge(sh, v)
            ins = fn(e)
            if h is not None:
                ins.then_inc(h, amt)

        self.q[eng].append(run)
        self.log[eng].append(([(id(s_), v_) for (s_, v_) in [(w[0], w[1]) for w in waits if w is not None] if v_ > 0],
                              (id(inc), amt) if inc is not None else None, len(self.log[eng])))
        if inc is not None:
            return (inc, inc.v)
        return None

    def eng_sems(self):
        self.es = {e: self.sem("es_" + e) for e in ("pe", "act", "dve", "pool")}

    def x(self, eng, fns, reads=(), writes=(), waits=()):
        if callable(fns):
            fns = [fns]
        ws = [w for w in waits if w is not None]
        for b in reads:
            ws.append(b.w)
        for b in writes:
            ws.append(b.w)
            ws.extend(b.rd.values())
        res = None
        for k, fn in enumerate(fns):
            last = k == len(fns) - 1
            res = self.op(eng, fn, waits=ws if k == 0 else (), inc=self.es[eng] if last else None)
        for b in reads:
            b.rd[eng] = res
        for b in writes:
            b.w = res
            b.rd = {}
        return res

    def xdma(self, eng, out, in_, sem, reads=(), writes=(), waits=()):
        ws = [w for w in waits if w is not None]
        for b in reads:
            ws.append(b.w)
        for b in writes:
            ws.append(b.w)
            ws.extend(b.rd.values())
        res = self.dma(eng, out, in_, waits=ws, inc=sem)
        for b in reads:
            b.rd[("dma", id(sem))] = res
        for b in writes:
            b.w = res
            b.rd = {}
        return res

    def dma(self, eng, out, in_, waits=(), inc=None):
        return self.op(eng, lambda e: e.dma_start(out=out, in_=in_), waits=waits, inc=inc, n=16)

    def wait_only(self, eng, waits):
        ws = []
        for s, v in waits:
            if v > 0 and self.waited[eng].get(id(s), 0) < v:
                self.waited[eng][id(s)] = v
                ws.append((s.h, v))

        def run(e, ws=ws):
            for sh, v in ws:
                e.wait_ge(sh, v)

        if ws:
            self.q[eng].append(run)
            self.log[eng].append(([(id(s_), v_) for (s_, v_) in waits if v_ > 0], None, len(self.log[eng])))


class Ctx:
    pass


def build_program(nphases=99, skip=0):
    nc = bass.Bass("TRN2", target_bir_lowering=False)
    stack = contextlib.ExitStack()
    kb = KB(nc, stack)
    c = Ctx()
    c.nc, c.kb = nc, kb

    def din(name, shape, dt=F32):
        return nc.dram_tensor(name, list(shape), dt, kind="ExternalInput").ap()

    def dscr(name, shape, dt):
        return nc.dram_tensor(name, list(shape), dt, kind="Internal").ap()

    I = {}
    I["x"] = din("x", [S, D])
    for p in ("ffn1", "ffn2"):
        I[p + "_norm"] = din(p + "_norm", [DEPTH, D])
        I[p + "_w_gate"] = din(p + "_w_gate", [DEPTH, D, DFF])
        I[p + "_w_up"] = din(p + "_w_up", [DEPTH, D, DFF])
        I[p + "_w_down"] = din(p + "_w_down", [DEPTH, DFF, D])
    I["mix_norm"] = din("mix_norm", [DEPTH, D])
    I["even_w_in"] = din("even_w_in", [2, D, INW])
    I["sgu_norm"] = din("sgu_norm", [2, 4, 128])
    I["sgu_w_spatial"] = din("sgu_w_spatial", [2, 4, 128, 128])
    I["sgu_b_spatial"] = din("sgu_b_spatial", [2, 4, 128])
    I["attn_q_norm"] = din("attn_q_norm", [2, 64])
    I["attn_k_norm"] = din("attn_k_norm", [2, 64])
    I["even_w_out"] = din("even_w_out", [2, D, D])
    I["pool_w_group"] = din("pool_w_group", [2, 4, 256, 256])
    I["pool_scale"] = din("pool_scale", [2, D])
    I["c_ident"] = din("c_ident", [128, 128], BF16)
    I["c_pool"] = din("c_pool", [128, 4, 5, 128], BF16)
    I["c_mask"] = din("c_mask", [128, 512], BF16)
    I["c_cos"] = din("c_cos", [S, 64])
    I["c_sin"] = din("c_sin", [S, 64])
    out = nc.dram_tensor("out", [S, D], F32, kind="ExternalOutput").ap()
    c.I, c.out = I, out

    Wb = {}
    for p in ("ffn1", "ffn2"):
        Wb[p + "_w_gate"] = dscr(p + "_wg_b", [DEPTH, D, DFF], BF16)
        Wb[p + "_w_up"] = dscr(p + "_wu_b", [DEPTH, D, DFF], BF16)
        Wb[p + "_w_down"] = dscr(p + "_wd_b", [DEPTH, DFF, D], BF16)
    Wb["even_w_in"] = dscr("win_b", [2, D, INW], BF16)
    Wb["even_w_out"] = dscr("wout_b", [2, D, D], BF16)
    Wb["pool_w_group"] = dscr("wpool_b", [2, 4, 256, 256], BF16)
    c.Wb = Wb
    c.vaug_d = dscr("vaug_d", [S + 2048 + 16, VW], BF16)
    c.o_d = dscr("o_d", [3, S + 16, OW], F32)
    c.aoT_d = dscr("aoT_d", [512, S], BF16)

    NW = 53000
    arena_t = stack.enter_context(nc.sbuf_tensor("arena", [128, NW], F32))
    psum_t = stack.enter_context(nc.psum_tensor("psum", [128, 4096], F32))
    c.A = Arena(arena_t[:], NW)
    c.ps = psum_t[:]
    c.bank = lambda i: c.ps[:, i * 512:(i + 1) * 512]

    c.ident = c.A.bf16([128])
    c.tiny = c.A.f32([8])
    c.cst = c.A.f32([8])
    c.nh = c.A.f32([16])
    c.mask2 = c.A.bf16([512])
    c.zt = c.A.bf16([VW])
    c.A.freeze()
    c.s_const = kb.sem("const", persistent=True)
    c.s_cst = kb.sem("cst", persistent=True)
    kb.op("pool", lambda e: e.memset(c.cst[:, 0:1], -0.5), inc=c.s_cst)
    kb.op("pool", lambda e: e.memset(c.cst[:, 1:2], EPS), inc=c.s_cst)
    kb.op("pool", lambda e: e.memset(c.nh, -0.5), inc=c.s_cst)
    kb.op("pool", lambda e: e.memset(c.tiny, 0.0), inc=c.s_cst)
    rz = kb.op("pool", lambda e: e.memset(c.zt, 0.0), inc=c.s_cst)
    c.cst_ready = (c.s_cst, c.s_cst.v)
    kb.dma("sp", c.mask2, I["c_mask"], inc=c.s_const)
    for a_ in range(8):
        kb.dma("sp", c.vaug_d[0:1024, :].rearrange("(p a) f -> p a f", p=128)[:, a_, :], c.zt,
               waits=[rz] if a_ == 0 else [], inc=c.s_const)
        kb.dma("sp", c.vaug_d[1024 + S:2048 + S, :].rearrange("(p a) f -> p a f", p=128)[:, a_, :], c.zt, inc=c.s_const)
    c.const_ready = (c.s_const, c.s_const.v)
    kb.dma("sp", c.ident, I["c_ident"], inc=c.s_const)
    c.const_ready = (c.s_const, c.s_const.v)

    c.wready = {}
    phases = []
    for layer in range(DEPTH):
        phases.append(("ffn", "ffn1", layer))
        phases.append(("mix", None, layer))
        phases.append(("ffn", "ffn2", layer))

    castq = []

    def add_cast(dst, src, sm, nchunk):
        rows = dst.shape[0]
        step = rows // nchunk
        for k in range(nchunk):
            castq.append((dst[k * step:(k + 1) * step], src[k * step:(k + 1) * step], sm))
            sm.v += 16
        return (sm, sm.v)

    for pi, (kind, p, layer) in enumerate(phases):
        j = layer // 2
        if pi < skip:
            continue
        if kind == "ffn":
            sm_d = kb.sem(f"castd_{pi}", persistent=True)
            sm_g = kb.sem(f"castg_{pi}", persistent=True)
            rd = add_cast(Wb[p + "_w_down"][layer], I[p + "_w_down"][layer], sm_d, 8)
            add_cast(Wb[p + "_w_gate"][layer], I[p + "_w_gate"][layer], sm_g, 8)
            rg = add_cast(Wb[p + "_w_up"][layer], I[p + "_w_up"][layer], sm_g, 8)
            c.wready[pi] = ([rg] * (NFC // 2), rd)
        else:
            sm = kb.sem(f"cast_{pi}", persistent=True)
            if layer % 2 == 0:
                add_cast(Wb["even_w_in"][j], I["even_w_in"][j], sm, 8)
                c.wready[pi] = add_cast(Wb["even_w_out"][j], I["even_w_out"][j], sm, 8)
            else:
                c.wready[pi] = add_cast(Wb["pool_w_group"][j].rearrange("g d e -> (g d) e"),
                                        I["pool_w_group"][j].rearrange("g d e -> (g d) e"), sm, 2)
    st_cast = {"i": 0}

    def cast_pump(n):
        while n > 0 and st_cast["i"] < len(castq):
            dst, src, sm = castq[st_cast["i"]]
            st_cast["i"] += 1
            n -= 1
            h_ = sm.h
            kb.q["pool"].append(lambda e, dst=dst, src=src, h_=h_: e.dma_start(out=dst, in_=src).then_inc(h_, 16))
            kb.log["pool"].append(([], (id(sm), 16), len(kb.log["pool"])))

    c.cast_pump = cast_pump
    if skip > 0:
        cast_pump(len(castq))

    c.bar = kb.sem("bar", persistent=True)
    c.nbar = 0

    src = I["x"]
    if skip > 0:
        s_cp = kb.sem("cp")
        r_cp = kb.dma("sp", out, I["x"], inc=s_cp)
        kb.wait_only("sp", [r_cp])
        barrier(c)
        src = out
    for pi in range(skip, min(nphases, len(phases))):
        kind, p, layer = phases[pi]
        c.pi = pi
        if kind == "ffn":
            ffn_phase(c, src, out, I[p + "_norm"][layer], Wb[p + "_w_gate"][layer], Wb[p + "_w_up"][layer],
                      Wb[p + "_w_down"][layer], c.wready[pi])
            src = out
        elif layer % 2 == 1:
            if _os.environ.get("KOLDPOOL"):
                pool_phase(c, out, layer)
            else:
                pool_pipelined(c, out, layer)
        else:
            even_phase(c, out, layer)
        barrier(c)

    stuck, _ = kb.check_deadlock()
    if stuck:
        raise RuntimeError(f"deadlock in recorded program: {stuck}")
    print("arena high water (words)", c.A.hw, "of", NW)
    print("instr counts", {e: len(kb.q[e]) for e in KB.ENG}, "sems", kb.nsem)
    with nc.Block() as block:
        @block.tensor
        def _(e):
            for f in kb.q["pe"]:
                f(e)

        @block.scalar
        def _(e):
            for f in kb.q["act"]:
                f(e)

        @block.vector
        def _(e):
            for f in kb.q["dve"]:
                f(e)

        @block.gpsimd
        def _(e):
            for f in kb.q["pool"]:
                f(e)

        @block.sync
        def _(e):
            for f in kb.q["sp"]:
                f(e)
    stack.close()
    return nc


def barrier(c):
    kb = c.kb
    t = c.tiny
    for eng in KB.ENG:
        for sm in kb.phase_sems:
            kb.wait_only(eng, [(sm, sm.v)])
    if _os.environ.get("KDRAIN"):
        for eng in KB.ENG:
            kb.q[eng].append(lambda e: e.drain())
            kb.log[eng].append(([], None, len(kb.log[eng])))
    kb.op("act", lambda e: e.activation(out=t[:, 0:1], in_=t[:, 1:2], func=AF.Copy), waits=[c.cst_ready], inc=c.bar)
    kb.op("dve", lambda e: e.tensor_copy(out=t[:, 2:3], in_=t[:, 3:4]), waits=[c.cst_ready], inc=c.bar)
    kb.op("pool", lambda e: e.tensor_copy(out=t[:, 4:5], in_=t[:, 5:6]), waits=[c.cst_ready], inc=c.bar)
    kb.op("pe", lambda e: e.matmul(c.ps[0:1, 4095:4096], c.ident[0:1, 0:1], c.ident[0:1, 0:1], start=True, stop=True),
          waits=[c.const_ready], inc=c.bar)
    kb.op("sp", lambda e: e.sem_inc(c.bar.h, 1))
    c.bar.v += 1
    w_, i_, n_ = kb.log["sp"][-1]
    kb.log["sp"][-1] = (w_, (id(c.bar), 1), n_)
    for eng in KB.ENG:
        kb.wait_only(eng, [(c.bar, c.bar.v)])
    sems = list(kb.phase_sems)
    if sems:
        finals = [(sm.h, sm.v) for sm in sems if sm.v > 0]

        def clr(e, sems=sems, finals=finals):
            for h_, v_ in finals:
                e.wait_ge(h_, v_)
            for sm in sems:
                e.sem_clear(sm.h)
            e.sem_inc(c.bar.h, 1)
        kb.q["sp"].append(clr)
        kb.log["sp"].append(([], ([id(sm) for sm in sems], "clear"), len(kb.log["sp"])))
        kb.log["sp"].append(([], (id(c.bar), 1), len(kb.log["sp"])))
        c.bar.v += 1
        for eng in KB.ENG:
            kb.wait_only(eng, [(c.bar, c.bar.v)])
        for sm in sems:
            sm.v = 0
            for eng in KB.ENG:
                kb.waited[eng].pop(id(sm), None)
    kb.end_phase()


def ffn_phase(c, x_src, x_dst, gain_ap, wg_b, wu_b, wd_b, wready):
    kb, A = c.kb, c.A
    A.reset()
    T = 1024
    NT = S // T
    NBLK = T // 128
    NG = NFC // 2
    gain_bc = A.f32([D])
    xblk = [A.f32([D]) for _ in range(4)]
    ss = A.f32([4])
    rstd = A.f32([4])
    mse = A.f32([4])
    sg = [A.f32([512]) for _ in range(2)]
    sqjunk = A.bf16([D])
    hblk = [A.bf16([D]) for _ in range(2)]
    hT = [A.bf16([8, T]) for _ in range(2)]
    wg = [A.bf16([8, 256]) for _ in range(3)]
    wu = [A.bf16([8, 256]) for _ in range(3)]
    wd = A.bf16([NFC, D])
    aT = A.bf16([NFC, T])
    tp = c.bank(0).bitcast(BF16)[:, 0:1024].rearrange("p (a b) -> p a b", a=8, b=128)

    s_xload = [kb.sem("xload") for _ in range(4)]
    s_xstore = [kb.sem("xstore") for _ in range(4)]
    s_wload = [kb.sem("wload") for _ in range(3)]
    s_misc = kb.sem("misc")
    s_sq, s_rstd, s_h, s_tp, s_tpc = (kb.sem(n) for n in ("sq", "rstd", "h", "tp", "tpc"))
    s_mse = kb.sem("mse")
    s_gu, s_sg, s_a, s_dn, s_y = (kb.sem(n) for n in ("gu", "sg", "a", "dn", "y"))

    kb.dma("sp", gain_bc, gain_ap.partition_broadcast(128), inc=s_misc)
    wd_src = wd_b.rearrange("(c p) d -> p c d", p=128)
    wr_groups, wr_d = wready
    gain_ready = (s_misc, s_misc.v)
    s_wd = kb.sem("wd")
    wd_ready = (s_wd, 32)

    def load_wd():
        kb.dma("sp", wd[:, 0:11, :], wd_src[:, 0:11, :], waits=[wr_d], inc=s_wd)
        kb.dma("sp", wd[:, 11:22, :], wd_src[:, 11:22, :], inc=s_wd)
    misc_ready = gain_ready

    st = Ctx()
    st.xi = 0
    st.xfree = []
    st.xitems = {}
    st.wi = 0
    st.wlast = []
    st.nrm = 0
    st.gu = 0
    st.dn = 0
    st.tpc_tile = {}

    def load_x(key, rows, src):
        i = st.xi
        st.xi += 1
        slot = i % 4
        waits = [st.xfree[i - 4]] if i >= 4 else []
        r = kb.dma("sp", xblk[slot], src[rows * 128:(rows + 1) * 128, :], waits=waits, inc=s_xload[slot])
        st.xitems[key] = (i, slot, r)
        st.xfree.append(None)

    wg_src = wg_b.rearrange("(c p) f -> p c f", p=128)
    wu_src = wu_b.rearrange("(c p) f -> p c f", p=128)

    fast_start = (c.pi == 0)
    if fast_start:
        stg = [A.f32([8, 256]) for _ in range(3)]
        s_stg = [kb.sem("stg") for _ in range(3)]
        s_castA, s_castD = kb.sem("castA"), kb.sem("castD")
        st.stg_i = 0
        st.stg_free = []
        w32 = [c.I["ffn1_w_gate"][0].rearrange("(c p) f -> p c f", p=128),
               c.I["ffn1_w_up"][0].rearrange("(c p) f -> p c f", p=128)]

    def load_w_fast(g, i, slot):
        rs = []
        war = [(s_gu, st.wlast[i - 3])] if i >= 3 else []
        for k in range(2):
            j = st.stg_i
            st.stg_i += 1
            tile_ = stg[j % 3]
            wfree = [st.stg_free[j - 3]] if j >= 3 else []
            r_l = kb.dma("sp", tile_, w32[k][:, :, g * 256:(g + 1) * 256], waits=wfree, inc=s_stg[j % 3])
            if k == 0:
                r = kb.op("act", lambda e, tile_=tile_: e.activation(out=wg[slot], in_=tile_, func=AF.Copy),
                          waits=[r_l] + war, inc=s_castA)
            else:
                r = kb.op("dve", lambda e, tile_=tile_: e.tensor_copy(out=wu[slot], in_=tile_),
                          waits=[r_l] + war, inc=s_castD)
            st.stg_free.append(r)
            rs.append(r)
        st.wlast.append(None)
        return (i, slot, rs)

    def load_w(g):
        i = st.wi
        st.wi += 1
        slot = i % 3
        if fast_start and i < 2 * NG:
            return load_w_fast(g, i, slot)
        waits = [wr_groups[g]]
        if i >= 3:
            waits.append((s_gu, st.wlast[i - 3]))
        kb.dma("sp", wg[slot], wg_src[:, :, g * 256:(g + 1) * 256], waits=waits, inc=s_wload[slot])
        r = kb.dma("sp", wu[slot], wu_src[:, :, g * 256:(g + 1) * 256], inc=s_wload[slot])
        st.wlast.append(None)
        return (i, slot, r)

    st.pend = []

    def norm_unit(tt, b):
        i = st.nrm
        st.nrm += 1
        xi, slot, xr = st.xitems[("n", tt, b)]
        si = i % 4
        hs = i % 2
        xb = xblk[slot]
        r_sq = kb.op("act", lambda e: e.activation(out=sqjunk, in_=xb, func=AF.Square, scale=1.0 / 32.0,
                                                   accum_out=ss[:, si:si + 1]),
                     waits=[xr, (s_mse, i - 3), (s_sq, i)], inc=s_sq)
        r_e = kb.op("pool", lambda e: e.tensor_tensor(out=mse[:, si:si + 1], in0=ss[:, si:si + 1],
                                                      in1=c.cst[:, 1:2], op=ALU.add),
                    waits=[r_sq, c.cst_ready, (s_rstd, i - 3)], inc=s_mse)
        r_rs = kb.op("pool", lambda e: e.tensor_tensor(out=rstd[:, si:si + 1], in0=mse[:, si:si + 1],
                                                       in1=c.cst[:, 0:1], op=ALU.pow),
                     waits=[r_e, (s_h, i - 3)], inc=s_rstd)
        r_h = kb.op("dve", lambda e: e.scalar_tensor_tensor(out=hblk[hs], in0=xb, scalar=rstd[:, si:si + 1],
                                                            in1=gain_bc, op0=ALU.mult, op1=ALU.mult),
                    waits=[r_rs, misc_ready, (s_tp, i - 1)], inc=s_h)
        st.xfree[xi] = r_h
        st.pend.append((tt, b, i, hs, r_h))

    def norm_flush(n=None):
        while st.pend and (n is None or n > 0):
            tt, b, i, hs, r_h = st.pend.pop(0)
            if n is not None:
                n -= 1
            buf = tt % 2
            for dc in range(8):
                last = dc == 7
                r_tp = kb.op("pe", lambda e, dc=dc, hs=hs: e.transpose(out=tp[:, dc, :],
                                                                       in_=hblk[hs][:, dc * 128:(dc + 1) * 128],
                                                                       identity=c.ident),
                             waits=[r_h, (s_tpc, i), c.const_ready] if dc == 0 else [],
                             inc=s_tp if last else None)
            r_c = kb.op("act", lambda e, buf=buf, b=b: e.activation(out=hT[buf][:, :, b * 128:(b + 1) * 128], in_=tp,
                                                                    func=AF.Copy),
                        waits=[r_tp], inc=s_tpc)
            st.tpc_tile[tt] = r_c

    def gu_unit(tt, fc, half, wslot, wr):
        j = st.gu
        st.gu += 1
        pb = j % 2
        gps = c.bank(1 + 2 * pb)
        ups = c.bank(2 + 2 * pb)
        buf = tt % 2
        fo = (fc % 2) * 128
        for which, (w, pst) in enumerate(((wg[wslot], gps), (wu[wslot], ups))):
            for dc in range(8):
                first = which == 0 and dc == 0
                last = which == 1 and dc == 7
                r_mm = kb.op("pe", lambda e, w=w, pst=pst, dc=dc: e.matmul(
                    pst, w[:, dc, fo:fo + 128], hT[buf][:, dc, half * 512:(half + 1) * 512],
                    start=(dc == 0), stop=(dc == 7)),
                    waits=(list(wr) if isinstance(wr, list) else [wr]) + [st.tpc_tile[tt], (s_a, j - 1)] if first else [],
                    inc=s_gu if last else None)
        r_sg = kb.op("act", lambda e: e.activation(out=sg[pb], in_=gps, func=AF.Silu),
                     waits=[r_mm, (s_a, j - 1)], inc=s_sg)
        kb.op("dve", lambda e: e.tensor_tensor(out=aT[:, fc, half * 512:(half + 1) * 512], in0=sg[pb], in1=ups,
                                               op=ALU.mult),
              waits=[r_sg, r_mm], inc=s_a)

    def dn_unit(tt, b, dh):
        j = st.dn
        st.dn += 1
        yps = c.bank(5 + j % 2)
        xi, slot, xr = st.xitems[("r", tt, b)]
        for fc in range(NFC):
            r_mm = kb.op("pe", lambda e, fc=fc: e.matmul(
                yps, aT[:, fc, b * 128:(b + 1) * 128], wd[:, fc, dh * 512:(dh + 1) * 512],
                start=(fc == 0), stop=(fc == NFC - 1)),
                waits=[(s_a, 2 * NFC * (tt + 1)), (s_y, j - 1), wd_ready] if fc == 0 else [],
                inc=s_dn if fc == NFC - 1 else None)
        xs = xblk[slot][:, dh * 512:(dh + 1) * 512]
        r_y = kb.op("dve", lambda e: e.scalar_tensor_tensor(out=xs, in0=yps, scalar=0.5, in1=xs,
                                                            op0=ALU.mult, op1=ALU.add),
                    waits=[r_mm, xr], inc=s_y)
        if dh == 1:
            gb = tt * NBLK + b
            r_st = kb.dma("sp", x_dst[gb * 128:(gb + 1) * 128, :], xblk[slot], waits=[r_y], inc=s_xstore[slot])
            st.xfree[xi] = r_st

    load_x(("n", 0, 0), 0, x_src)
    load_x(("n", 0, 1), 1, x_src)
    wq = [load_w(0), load_w(1)]
    for b in range(NBLK):
        if b + 2 < NBLK:
            load_x(("n", 0, b + 2), b + 2, x_src)
        norm_unit(0, b)
        if b >= 1:
            norm_flush(1)
    norm_flush()
    if not fast_start:
        load_wd()
    for tt in range(NT):
        nxt = tt + 1 < NT
        if nxt:
            load_x(("n", tt + 1, 0), (tt + 1) * NBLK, x_src)
        for g in range(NG):
            wi, wslot, wr = wq.pop(0)
            gg = tt * NG + g + 2
            if gg < NT * NG:
                wq.append(load_w(gg % NG))
            for fcl in range(2):
                for half in range(2):
                    gu_unit(tt, 2 * g + fcl, half, wslot, wr)
            st.wlast[wi] = st.gu
            c.cast_pump(2 if (fast_start and tt * NG + g < 12) else 1)
            if fast_start and tt == 0 and g == 5:
                load_wd()
            if nxt:
                norm_flush()
                if g < NBLK:
                    if g + 1 < NBLK:
                        load_x(("n", tt + 1, g + 1), (tt + 1) * NBLK + g + 1, x_src)
                    norm_unit(tt + 1, g)
        load_x(("r", tt, 0), tt * NBLK, x_src)
        for b in range(NBLK):
            if b + 1 < NBLK:
                load_x(("r", tt, b + 1), tt * NBLK + b + 1, x_src)
            dn_unit(tt, b, 0)
            dn_unit(tt, b, 1)
    kb.wait_only("sp", [(s, s.v) for s in s_xstore])


class NormStage:
    def __init__(self, c, gain_ap, nx=4):
        kb, A = c.kb, c.A
        self.c = c
        self.gain_bc = A.f32([D])
        self.ss = A.f32([4])
        self.mse = A.f32([4])
        self.rstd = A.f32([4])
        self.sqjunk = A.bf16([D])
        self.xring = Ring([A.f32([D]) for _ in range(nx)])
        self.s_xload = [kb.sem("nxload") for _ in range(nx)]
        self.s_sq, self.s_mse, self.s_rstd, self.s_h, self.s_g = (kb.sem(n) for n in ("nsq", "nmse", "nrstd", "nh", "ng"))
        kb.dma("sp", self.gain_bc, gain_ap.partition_broadcast(128), inc=self.s_g)
        self.g_ready = (self.s_g, self.s_g.v)
        self.i = 0

    def load(self, src_rows):
        kb = self.c.kb
        xi, slot, xb, w = self.xring.acquire()
        r = kb.dma("sp", xb, src_rows, waits=w, inc=self.s_xload[slot])
        return (xi, xb, r)

    def norm(self, xitem, hout, extra_waits=()):
        c, kb = self.c, self.c.kb
        xi, xb, xr = xitem
        i = self.i
        self.i += 1
        si = i % 4
        ss, mse, rstd = self.ss, self.mse, self.rstd
        r_sq = kb.op("act", lambda e: e.activation(out=self.sqjunk, in_=xb, func=AF.Square, scale=1.0 / 32.0,
                                                   accum_out=ss[:, si:si + 1]),
                     waits=[xr, (self.s_mse, i - 3), (self.s_sq, i)], inc=self.s_sq)
        r_e = kb.op("pool", lambda e: e.tensor_tensor(out=mse[:, si:si + 1], in0=ss[:, si:si + 1],
                                                      in1=c.cst[:, 1:2], op=ALU.add),
                    waits=[r_sq, c.cst_ready, (self.s_rstd, i - 3)], inc=self.s_mse)
        r_rs = kb.op("pool", lambda e: e.tensor_tensor(out=rstd[:, si:si + 1], in0=mse[:, si:si + 1],
                                                       in1=c.cst[:, 0:1], op=ALU.pow),
                     waits=[r_e, (self.s_h, i - 3)], inc=self.s_rstd)
        r_h = kb.op("dve", lambda e: e.scalar_tensor_tensor(out=hout, in0=xb, scalar=rstd[:, si:si + 1],
                                                            in1=self.gain_bc, op0=ALU.mult, op1=ALU.mult),
                    waits=[r_rs, self.g_ready] + list(extra_waits), inc=self.s_h)
        return r_h


def pool_pipelined(c, xd, layer):
    kb, A, I = c.kb, c.A, c.I
    j = layer // 2
    NB = S // 128
    A.reset()
    kb.eng_sems()
    NX = 9
    g = norm_setup(c, kb, A, I["mix_norm"][layer], nx=NX)
    hbs = [TB(A.bf16([D])) for _ in range(5)]
    mt = TB(A.bf16([4, 5, 128]))
    wp = TB(A.bf16([4, 2, 256]))
    scale_bc = TB(A.f32([D]))
    yT = [TB(A.bf16([8, 128])) for _ in range(2)]
    tmp = [TB(A.f32([D])) for _ in range(2)]
    s_c = kb.sem("pc")
    s_st = [kb.sem("pst") for _ in range(NX)]
    kb.dma("sp", mt.ap, I["c_pool"], inc=s_c)
    kb.dma("sp", scale_bc.ap, I["pool_scale"][j].partition_broadcast(128), inc=s_c)
    for gg in range(4):
        r_c = kb.dma("sp", wp.ap[:, gg, :, :], c.Wb["pool_w_group"][j][gg].rearrange("(c p) e -> p c e", p=128),
                     waits=[c.wready[c.pi]], inc=s_c)
    for tb_ in (mt, wp, scale_bc):
        tb_.w = r_c
    yps = [TB(c.ps[:, 0:1024].rearrange("p (a b) -> p a b", a=8, b=128)),
           TB(c.ps[:, 1024:2048].rearrange("p (a b) -> p a b", a=8, b=128))]
    zps = [TB(c.ps[:, 2048:3072]), TB(c.ps[:, 3072:4096])]

    def sl(b):
        xb = g["x"][b % NX]
        kb.xdma("sp", xb.ap, xd[b * 128:(b + 1) * 128, :], g["xl"][b % NX], writes=[xb])

    def s0(b):
        norm_ops(c, kb, g, b, g["x"][b % NX], hbs[b % 5])

    def sy(b):
        yp = yps[b % 2]
        sbs = [sb for sb in (b - 1, b, b + 1) if 0 <= sb < NB]
        fns = []
        for dc in range(8):
            gg = dc // 2
            for sb in sbs:
                if sb == b - 1:
                    kind = 0
                elif sb == b + 1:
                    kind = 2
                else:
                    kind = 3 if b == 0 else (4 if b == NB - 1 else 1)
                hb = hbs[sb % 5]
                fns.append(lambda e, dc=dc, gg=gg, kind=kind, hb=hb, first=(sb == sbs[0]), last=(sb == sbs[-1]): e.matmul(
                    yp.ap[:, dc, :], hb.ap[:, dc * 128:(dc + 1) * 128], mt.ap[:, gg, kind, :], start=first, stop=last))
        kb.x("pe", fns, reads=[hbs[sb % 5] for sb in sbs] + [mt], writes=[yp])

    def syc(b):
        kb.x("act", lambda e: e.activation(out=yT[b % 2].ap, in_=yps[b % 2].ap, func=AF.Copy), reads=[yps[b % 2]],
             writes=[yT[b % 2]])

    def sz(b):
        zp, yt = zps[b % 2], yT[b % 2]
        kb.x("pe", [lambda e, dc=dc: e.matmul(zp.ap[:, (dc // 2) * 256:(dc // 2 + 1) * 256], yt.ap[:, dc, :],
                                              wp.ap[:, dc // 2, dc % 2, :], start=(dc % 2 == 0), stop=(dc % 2 == 1))
                    for dc in range(8)], reads=[yt, wp], writes=[zp])

    def sfin(b):
        zp, tm, xb = zps[b % 2], tmp[b % 2], g["x"][b % NX]
        kb.x("dve", lambda e: e.tensor_tensor(out=tm.ap, in0=zp.ap, in1=scale_bc.ap, op=ALU.mult),
             reads=[zp, scale_bc], writes=[tm])
        kb.x("dve", lambda e: e.tensor_tensor(out=xb.ap, in0=tm.ap, in1=xb.ap, op=ALU.add), reads=[tm], writes=[xb])
        kb.xdma("sp", xd[b * 128:(b + 1) * 128, :], xb.ap, s_st[b % NX], reads=[xb])

    for t in range(NB + 7):
        for fn, lag in ((sfin, 6), (sz, 5), (syc, 4), (sy, 3), (s0, 1), (sl, 0)):
            i = t - lag
            if 0 <= i < NB:
                fn(i)
    kb.wait_only("sp", [(sm, sm.v) for sm in s_st])


def pool_phase(c, xd, layer):
    kb, A, I = c.kb, c.A, c.I
    j = layer // 2
    A.reset()
    NB = S // 128
    ns = NormStage(c, I["mix_norm"][layer], nx=5)
    hring = Ring([A.bf16([D]) for _ in range(5)])
    mt = A.bf16([4, 5, 128])
    wp = A.bf16([4, 2, 256])
    scale_bc = A.f32([D])
    yT = [A.bf16([8, 128]) for _ in range(2)]
    tmp = [A.f32([D]) for _ in range(2)]
    s_c = kb.sem("pc")
    kb.dma("sp", mt, I["c_pool"], inc=s_c)
    kb.dma("sp", scale_bc, I["pool_scale"][j].partition_broadcast(128), inc=s_c)
    for g in range(4):
        kb.dma("sp", wp[:, g, :, :], c.Wb["pool_w_group"][j][g].rearrange("(c p) e -> p c e", p=128),
               waits=[c.wready[c.pi]], inc=s_c)
    c_ready = (s_c, s_c.v)
    s_y, s_yc, s_z, s_t, s_o = (kb.sem(n) for n in ("py", "pyc", "pz", "pt", "po"))
    s_store = [kb.sem("pst") for _ in range(5)]
    yps = [c.ps[:, 0:1024].rearrange("p (a b) -> p a b", a=8, b=128),
           c.ps[:, 1024:2048].rearrange("p (a b) -> p a b", a=8, b=128)]
    zps = c.ps[:, 2048:3072]

    xitems = {}
    hitems = {}

    def do_norm(b):
        xitems[b] = ns.load(xd[b * 128:(b + 1) * 128, :])
        hi, hslot, hb, w = hring.acquire()
        r_h = ns.norm(xitems[b], hb, extra_waits=w)
        hitems[b] = (hi, hb, r_h)

    ystate = {}

    def do_y(b, u):
        yp = yps[u % 2]
        sbs = [sb for sb in (b - 1, b, b + 1) if 0 <= sb < NB]
        n_mm = 8 * len(sbs)
        k = 0
        for dc in range(8):
            g = dc // 2
            for sb in sbs:
                if sb == b - 1:
                    kind = 0
                elif sb == b + 1:
                    kind = 2
                else:
                    kind = 3 if b == 0 else (4 if b == NB - 1 else 1)
                hb = hitems[sb][1]
                k += 1
                r_y = kb.op("pe", lambda e, dc=dc, g=g, kind=kind, hb=hb, sb=sb: e.matmul(
                    yp[:, dc, :], hb[:, dc * 128:(dc + 1) * 128], mt[:, g, kind, :],
                    start=(sb == sbs[0]), stop=(sb == sbs[-1])),
                    waits=[hitems[x][2] for x in sbs] + [c_ready, (s_yc, u - 1)] if k == 1 else [],
                    inc=s_y if k == n_mm else None)
        if b - 1 >= 0:
            hring.release(hitems[b - 1][0], [r_y])
        if b == NB - 1:
            hring.release(hitems[b][0], [r_y])
        yt = yT[u % 2]
        r_yc = kb.op("act", lambda e: e.activation(out=yt, in_=yp, func=AF.Copy), waits=[r_y, (s_z, u - 1)], inc=s_yc)
        ystate[b] = (yt, r_yc)

    def do_z(b, u):
        yt, r_yc = ystate.pop(b)
        for dc in range(8):
            g, dcl = dc // 2, dc % 2
            r_z = kb.op("pe", lambda e, dc=dc, g=g, dcl=dcl: e.matmul(
                zps[:, g * 256:(g + 1) * 256], yt[:, dc, :], wp[:, g, dcl, :], start=(dcl == 0), stop=(dcl == 1)),
                waits=[r_yc, (s_o, u)] if dc == 0 else [], inc=s_z if dc == 7 else None)
        tm = tmp[u % 2]
        xi, xb, xr = xitems[b]
        r_t = kb.op("dve", lambda e: e.tensor_tensor(out=tm, in0=zps, in1=scale_bc, op=ALU.mult),
                    waits=[r_z, c_ready, (s_o, u - 1)], inc=s_t)
        r_o = kb.op("dve", lambda e: e.tensor_tensor(out=xb, in0=tm, in1=xb, op=ALU.add), waits=[r_t], inc=s_o)
        slot = xi % 5
        r_st = kb.dma("sp", xd[b * 128:(b + 1) * 128, :], xb, waits=[r_o], inc=s_store[slot])
        ns.xring.release(xi, [r_st])

    do_norm(0)
    do_norm(1)
    for b in range(NB):
        if b + 2 < NB:
            do_norm(b + 2)
        do_y(b, b)
        if b >= 1:
            do_z(b - 1, b - 1)
    do_z(NB - 1, NB - 1)
    kb.wait_only("sp", [(sm, sm.v) for sm in s_store])


def even_phase(c, xd, layer):
    c.A.reset()
    e1a_pipelined(c, xd, layer)
    barrier(c)
    if DEBUG_STOP in ("e1a", "e1a0", "e1a1"):
        return
    even_attn(c, xd, layer, None)

def norm_ops(c, kb, g, i, xb, hb):
    si = i % 3
    ss, mse, rstd, junk = g["ss"][si], g["mse"][si], g["rstd"][si], g["junk"]
    kb.x("act", lambda e: e.activation(out=junk.ap, in_=xb.ap, func=AF.Square, scale=1.0 / 32.0, accum_out=ss.ap),
         reads=[xb], writes=[ss, junk])
    kb.x("pool", lambda e: e.tensor_tensor(out=mse.ap, in0=ss.ap, in1=c.cst[:, 1:2], op=ALU.add),
         reads=[ss], writes=[mse], waits=[c.cst_ready])
    kb.x("pool", lambda e: e.tensor_tensor(out=rstd.ap, in0=mse.ap, in1=c.cst[:, 0:1], op=ALU.pow),
         reads=[mse], writes=[rstd])
    kb.x("dve", lambda e: e.scalar_tensor_tensor(out=hb.ap, in0=xb.ap, scalar=rstd.ap, in1=g["gain"].ap,
                                                 op0=ALU.mult, op1=ALU.mult),
         reads=[xb, rstd, g["gain"]], writes=[hb])


def norm_setup(c, kb, A, gain_ap, nx=3):
    g = {}
    g["gain"] = TB(A.f32([D]))
    g["ss"] = [TB(A.f32([1])) for _ in range(3)]
    g["mse"] = [TB(A.f32([1])) for _ in range(3)]
    g["rstd"] = [TB(A.f32([1])) for _ in range(3)]
    g["junk"] = TB(A.bf16([D]))
    g["x"] = [TB(A.f32([D])) for _ in range(nx)]
    g["xl"] = [kb.sem("xl") for _ in range(nx)]
    g["sg"] = kb.sem("gn")
    kb.xdma("sp", g["gain"].ap, gain_ap.partition_broadcast(128), g["sg"], writes=[g["gain"]])
    return g


def e1b_pipelined(c, xd, layer, qz, kT, PAD):
    kb, A, I = c.kb, c.A, c.I
    j = layer // 2
    NB = S // 128
    kb.eng_sems()
    win_src = c.Wb["even_w_in"][j].rearrange("(c p) f -> p c f", p=128)
    g = norm_setup(c, kb, A, I["mix_norm"][layer], nx=3)
    hb = [TB(A.bf16([D])) for _ in range(2)]
    hT = [TB(A.bf16([8, 128])) for _ in range(2)]
    win = TB(A.bf16([8, 1536]))
    gq = TB(A.f32([2, 2, 64]))
    cs = [TB(A.f32([2, 64])) for _ in range(3)]
    cg = [[TB(A.f32([2, 64])) for _ in range(4)] for _ in range(2)]
    xf = [[TB(A.f32([8, 64])) for _ in range(2)] for _ in range(2)]
    sq = [[TB(A.f32([8, 64])) for _ in range(2)] for _ in range(2)]
    ssq, mseq, rsq = ([TB(A.f32([8])) for _ in range(2)] for _ in range(3))
    t1 = [TB(A.f32([8, 64])) for _ in range(2)]
    t2 = [TB(A.f32([8, 64])) for _ in range(2)]
    qkb = [[TB(A.bf16([8, 64])) for _ in range(2)] for _ in range(2)]
    vs = [TB(A.bf16([8, VH])) for _ in range(2)]
    s_c = kb.sem("e2c")
    s_cs = [kb.sem("cs") for _ in range(3)]
    s_vst = [kb.sem("vst") for _ in range(2)]
    for k3 in range(3):
        kb.xdma("sp", win.ap[:, :, k3 * 512:(k3 + 1) * 512], win_src[:, :, 1024 + k3 * 512:1536 + k3 * 512], s_c,
                writes=[win] if k3 == 2 else [], waits=[c.wready[c.pi]])
    for wq_, nm_ in enumerate(("attn_q_norm", "attn_k_norm")):
        kb.dma("sp", gq.ap[:, wq_, 0, :], I[nm_][j].partition_broadcast(128), inc=s_c)
        kb.dma("sp", gq.ap[:, wq_, 1, 0:32], I[nm_][j][32:64].partition_broadcast(128), inc=s_c)
        kb.xdma("sp", gq.ap[:, wq_, 1, 32:64], I[nm_][j][0:32].partition_broadcast(128), s_c, writes=[gq])
    win.w = gq.w
    kb.x("pool", lambda e: e.memset(qz[0][64:128, :, :], 0.0))
    kb.x("pool", lambda e: e.memset(qz[1][0:64, :, :], 0.0))
    kb.x("pool", lambda e: e.memset(kT[:, :, 0:PAD], 0.0))
    kb.x("pool", lambda e: e.memset(kT[:, :, PAD + S:PAD + S + PAD], 0.0))
    for v_ in vs:
        kb.x("pool", lambda e, v_=v_: e.memset(v_.ap, 0.0), writes=[v_])
        kb.x("pool", lambda e, v_=v_: e.memset(v_.ap[:, :, 64:65], 1.0), writes=[v_])
    r_init = kb.x("pool", lambda e: e.memset(c.tiny[:, 6:7], 0.0))
    tp = TB(c.bank(0).bitcast(BF16)[:, 0:1024].rearrange("p (a b) -> p a b", a=8, b=128))
    zq = [TB(c.bank(1)), TB(c.bank(2))]
    zv = TB(c.bank(3))
    tqs = [[TB(c.bank(4 + 2 * par + w_).bitcast(BF16)[:, 0:512].rearrange("p (a b) -> p a b", a=4, b=128))
            for w_ in range(2)] for par in range(2)]

    def sl(b):
        nx = len(g["x"])
        xb = g["x"][b % nx]
        kb.xdma("sp", xb.ap, xd[b * 128:(b + 1) * 128, :], g["xl"][b % nx], writes=[xb])
        csb = cs[b % 3]
        kb.dma("sp", csb.ap[:, 0, :], I["c_cos"][b * 128:(b + 1) * 128, :],
               waits=[csb.w] + list(csb.rd.values()), inc=s_cs[b % 3])
        kb.xdma("sp", csb.ap[:, 1, :], I["c_sin"][b * 128:(b + 1) * 128, :], s_cs[b % 3], writes=[csb])

    def s0a(b):
        nx = len(g["x"])
        xb = g["x"][b % nx]
        norm_ops(c, kb, g, b, xb, hb[b % 2])
        csb = cs[b % 3]
        for which in range(2):
            cgb = cg[which][b % 4]
            kb.x("pool", lambda e, cgb=cgb, which=which: e.tensor_tensor(out=cgb.ap, in0=csb.ap, in1=gq.ap[:, which, :, :],
                                                                          op=ALU.mult), reads=[csb, gq], writes=[cgb])

    def s0b(b):
        h = hb[b % 2]
        kb.x("pe", [lambda e, dc=dc: e.transpose(out=tp.ap[:, dc, :], in_=h.ap[:, dc * 128:(dc + 1) * 128], identity=c.ident)
                    for dc in range(8)], reads=[h], writes=[tp], waits=[c.const_ready])
        kb.x("act", lambda e: e.activation(out=hT[b % 2].ap, in_=tp.ap, func=AF.Copy), reads=[tp], writes=[hT[b % 2]])

    def s1(b):
        hTb = hT[b % 2]
        kb.x("pe", [lambda e, dc=dc: e.matmul(zv.ap, hTb.ap[:, dc, :], win.ap[:, dc, 1024:1536], start=(dc == 0), stop=(dc == 7))
                    for dc in range(8)], reads=[hTb, win], writes=[zv])
        vsb = vs[b % 2]
        kb.x("act", lambda e: e.activation(out=vsb.ap[:, :, 0:64], in_=zv.ap.rearrange("p (h d) -> p h d", h=8, d=64),
                                           func=AF.Copy), reads=[zv], writes=[vsb])
        kb.xdma("sp", c.vaug_d[PAD + b * 128:PAD + (b + 1) * 128, :], vsb.ap.rearrange("p h d -> p (h d)"),
                s_vst[b % 2], reads=[vsb])
        for which in range(2):
            zp = zq[which]
            co = which * 512
            kb.x("pe", [lambda e, dc=dc, zp=zp, co=co: e.matmul(zp.ap, hTb.ap[:, dc, :], win.ap[:, dc, co:co + 512],
                                                                start=(dc == 0), stop=(dc == 7)) for dc in range(8)],
                 reads=[hTb, win], writes=[zp])
            zp3 = zp.ap.rearrange("p (h d) -> p h d", h=8, d=64)
            xfb, sqb = xf[which][b % 2], sq[which][b % 2]
            kb.x("act", lambda e, zp3=zp3, xfb=xfb: e.activation(out=xfb.ap, in_=zp3, func=AF.Copy), reads=[zp], writes=[xfb])
            kb.x("act", lambda e, zp3=zp3, sqb=sqb: e.activation(out=sqb.ap, in_=zp3, func=AF.Square), reads=[zp], writes=[sqb])

    def s2(b):
        for which in range(2):
            sqb = sq[which][b % 2]
            ssq_, mse_, rsq_ = ssq[which], mseq[which], rsq[which]
            kb.x("dve", lambda e, sqb=sqb, ssq_=ssq_: e.tensor_reduce(out=ssq_.ap, in_=sqb.ap, axis=AX.X, op=ALU.add),
                 reads=[sqb], writes=[ssq_])
            kb.x("dve", lambda e, ssq_=ssq_, mse_=mse_: e.tensor_scalar(out=mse_.ap, in0=ssq_.ap, scalar1=1.0 / 64.0,
                                                                       scalar2=EPS, op0=ALU.mult, op1=ALU.add),
                 reads=[ssq_], writes=[mse_])
            kb.x("pool", lambda e, mse_=mse_, rsq_=rsq_: e.tensor_tensor(out=rsq_.ap, in0=mse_.ap, in1=c.nh[:, 0:8], op=ALU.pow),
                 reads=[mse_], writes=[rsq_], waits=[c.cst_ready])
        for which in range(2):
            xfb, qb, cgb = xf[which][b % 2], qkb[which][b % 2], cg[which][b % 4]
            rsq_, t1_, t2_ = rsq[which], t1[which], t2[which]
            kb.x("dve", lambda e, xfb=xfb, rsq_=rsq_: e.tensor_tensor(out=xfb.ap, in0=xfb.ap,
                                                                     in1=rsq_.ap.unsqueeze(2).to_broadcast([128, 8, 64]),
                                                                     op=ALU.mult), reads=[rsq_], writes=[xfb])
            cb = cgb.ap[:, 0, :].unsqueeze(1).to_broadcast([128, 8, 64])
            sb0 = cgb.ap[:, 1, 0:32].unsqueeze(1).to_broadcast([128, 8, 32])
            sb1 = cgb.ap[:, 1, 32:64].unsqueeze(1).to_broadcast([128, 8, 32])
            kb.x("dve", lambda e, xfb=xfb, cb=cb, t1_=t1_: e.tensor_tensor(out=t1_.ap, in0=xfb.ap, in1=cb, op=ALU.mult),
                 reads=[xfb, cgb], writes=[t1_])
            kb.x("dve", [lambda e, xfb=xfb, sb0=sb0, t2_=t2_: e.tensor_tensor(out=t2_.ap[:, :, 0:32], in0=xfb.ap[:, :, 32:64],
                                                                             in1=sb0, op=ALU.mult),
                         lambda e, xfb=xfb, sb1=sb1, t2_=t2_: e.tensor_tensor(out=t2_.ap[:, :, 32:64], in0=xfb.ap[:, :, 0:32],
                                                                             in1=sb1, op=ALU.mult)],
                 reads=[xfb, cgb], writes=[t2_])
            kb.x("dve", lambda e, qb=qb, t1_=t1_, t2_=t2_: e.tensor_tensor(out=qb.ap, in0=t1_.ap, in1=t2_.ap, op=ALU.add),
                 reads=[t1_, t2_], writes=[qb])

    def s3a(b):
        for which in range(2):
            qb = qkb[which][b % 2]
            tq = tqs[b % 2][which]
            qbf = qb.ap.rearrange("p h d -> p (h d)")
            kb.x("pe", [lambda e, hp=hp, qbf=qbf, tq=tq: e.transpose(out=tq.ap[:, hp, :], in_=qbf[:, hp * 128:(hp + 1) * 128],
                                                                     identity=c.ident) for hp in range(4)],
                 reads=[qb], writes=[tq])

    def s3b(b):
        tq = tqs[b % 2][0]
        kb.x("act", [lambda e: e.activation(out=qz[0][0:64, :, b * 128:(b + 1) * 128], in_=tq.ap[0:64, :, :], func=AF.Copy),
                     lambda e: e.activation(out=qz[1][64:128, :, b * 128:(b + 1) * 128], in_=tq.ap[64:128, :, :],
                                            func=AF.Copy)], reads=[tq], waits=[r_init])
        tk = tqs[b % 2][1]
        kb.x("act", lambda e: e.activation(out=kT[:, :, PAD + b * 128:PAD + (b + 1) * 128], in_=tk.ap, func=AF.Copy),
             reads=[tk], waits=[r_init])

    pipeline(NB, [sl, s0a, s0b, s1, s2, s3a, s3b])
    kb.wait_only("sp", [(sm, sm.v) for sm in s_vst])


def e1b_old(c, xd, layer, qz, kT, PAD, win_src):
    kb, A, I = c.kb, c.A, c.I
    j = layer // 2
    NB = S // 128
    A.reset()
    ns = NormStage(c, I["mix_norm"][layer], nx=2)
    hblk = Ring([A.bf16([D]) for _ in range(2)])
    hT = [A.bf16([8, 128]) for _ in range(2)]
    win = A.bf16([8, 1536])
    gq_bc = A.f32([2, 64])
    cs = [A.f32([2, 64]) for _ in range(2)]
    xf = A.f32([8, 64])
    sq = A.f32([8, 64])
    ssq = A.f32([8])
    mseq = A.f32([8])
    rsq = A.f32([8])
    xn = xf
    xg = xf
    t1 = A.f32([8, 64])
    t2 = A.f32([8, 64])
    qkb = [A.bf16([8, 64]) for _ in range(2)]
    vs = [A.bf16([8, VH]) for _ in range(2)]
    s_c = kb.sem("e2c")
    kb.dma("sp", win[:, :, 0:512], win_src[:, :, 1024:1536], inc=s_c)
    kb.dma("sp", win[:, :, 512:1024], win_src[:, :, 1536:2048], inc=s_c)
    kb.dma("sp", win[:, :, 1024:1536], win_src[:, :, 2048:2560], inc=s_c)
    kb.dma("sp", gq_bc[:, 0, :], I["attn_q_norm"][j].partition_broadcast(128), inc=s_c)
    kb.dma("sp", gq_bc[:, 1, :], I["attn_k_norm"][j].partition_broadcast(128), inc=s_c)
    c_ready = (s_c, s_c.v)
    s_i = kb.sem("e2i")
    kb.op("pool", lambda e: e.memset(qz[0][64:128, :, :], 0.0), inc=s_i)
    kb.op("pool", lambda e: e.memset(qz[1][0:64, :, :], 0.0), inc=s_i)
    kb.op("pool", lambda e: e.memset(kT[:, :, 0:PAD], 0.0), inc=s_i)
    kb.op("pool", lambda e: e.memset(kT[:, :, PAD + S:PAD + S + PAD], 0.0), inc=s_i)
    for v_ in vs:
        kb.op("pool", lambda e, v_=v_: e.memset(v_, 0.0), inc=s_i)
    for v_ in vs:
        kb.op("pool", lambda e, v_=v_: e.memset(v_[:, :, 64:65], 1.0), waits=[(s_i, 6)], inc=s_i)
    i_ready = (s_i, s_i.v)
    tp = c.bank(0).bitcast(BF16)[:, 0:1024].rearrange("p (a b) -> p a b", a=8, b=128)
    zps = [c.bank(1), c.bank(2)]
    vps = c.bank(3)
    tq = c.bank(4).bitcast(BF16)[:, 0:512].rearrange("p (a b) -> p a b", a=4, b=128)
    (s_tp, s_tpc, s_z, s_xf, s_sq, s_ss, s_ms, s_rs, s_xn, s_xg, s_t1, s_t2, s_qb, s_tq, s_tqc, s_vz, s_vc) = (
        kb.sem(n) for n in ("tp", "tpc", "z", "xf", "sq", "ss", "ms", "rs", "xn", "xg", "t1", "t2", "qb", "tq", "tqc",
                            "vz", "vc"))
    s_cs = [kb.sem("cs") for _ in range(3)]
    s_vst = [kb.sem("vst") for _ in range(2)]
    def qkv_block(b, nq0):
        nq = nq0
        hb_i = b % 2
        xi = ns.load(xd[b * 128:(b + 1) * 128, :])
        hi, hslot, hb, w = hblk.acquire()
        r_h = ns.norm(xi, hb, extra_waits=w)
        ns.xring.release(xi[0], [r_h])
        for dc in range(8):
            r_tp = kb.op("pe", lambda e, dc=dc, hb=hb: e.transpose(out=tp[:, dc, :], in_=hb[:, dc * 128:(dc + 1) * 128],
                                                                   identity=c.ident),
                         waits=[r_h, (s_tpc, b), c.const_ready] if dc == 0 else [], inc=s_tp if dc == 7 else None)
        hblk.release(hi, [r_tp])
        hTb = hT[hb_i]
        r_hT = kb.op("act", lambda e, hTb=hTb: e.activation(out=hTb, in_=tp, func=AF.Copy), waits=[r_tp], inc=s_tpc)
        csb = cs[b % 2]
        wcs = [(s_t2, 4 * (b - 1))] if b >= 2 else []
        kb.dma("sp", csb[:, 0, :], I["c_cos"][b * 128:(b + 1) * 128, :], waits=wcs, inc=s_cs[b % 2])
        r_cs = kb.dma("sp", csb[:, 1, :], I["c_sin"][b * 128:(b + 1) * 128, :], inc=s_cs[b % 2])
        for dc in range(8):
            r_vz = kb.op("pe", lambda e, dc=dc, hTb=hTb: e.matmul(vps, hTb[:, dc, :], win[:, dc, 1024:1536],
                                                                  start=(dc == 0), stop=(dc == 7)),
                         waits=[r_hT, c_ready, (s_vc, b)] if dc == 0 else [], inc=s_vz if dc == 7 else None)
        vsb = vs[b % 2]
        wv = [(s_vst[b % 2], 16 * (b // 2))] if b >= 2 else []
        r_vc = kb.op("act", lambda e, vsb=vsb: e.activation(out=vsb[:, :, 0:64],
                                                            in_=vps.rearrange("p (h d) -> p h d", h=8, d=64), func=AF.Copy),
                     waits=[r_vz, i_ready] + wv, inc=s_vc)
        kb.dma("sp", c.vaug_d[PAD + b * 128:PAD + (b + 1) * 128, :], vsb.rearrange("p h d -> p (h d)"),
               waits=[r_vc], inc=s_vst[b % 2])
        for which in range(2):
            qk_unit(b, which, nq, hTb, csb, r_cs)
            nq += 1

    def qk_unit(b, which, nq, hTb, csb, r_cs):
        if True:
            zp = zps[which]
            co = which * 512
            for dc in range(8):
                r_z = kb.op("pe", lambda e, dc=dc, zp=zp, co=co, hTb=hTb: e.matmul(
                    zp, hTb[:, dc, :], win[:, dc, co:co + 512], start=(dc == 0), stop=(dc == 7)),
                    waits=[(s_sq, nq - 1)] if dc == 0 else [], inc=s_z if dc == 7 else None)
            zp3 = zp.rearrange("p (h d) -> p h d", h=8, d=64)
            r_xf = kb.op("act", lambda e, zp3=zp3: e.activation(out=xf, in_=zp3, func=AF.Copy),
                         waits=[r_z, (s_xn, nq), (s_xg, nq), (s_t1, nq), (s_t2, 2 * nq)], inc=s_xf)
            r_sq = kb.op("act", lambda e, zp3=zp3: e.activation(out=sq, in_=zp3, func=AF.Square),
                         waits=[r_z, (s_ss, nq)], inc=s_sq)
            r_ss = kb.op("dve", lambda e: e.tensor_reduce(out=ssq, in_=sq, axis=AX.X, op=ALU.add),
                         waits=[r_sq, (s_ms, nq)], inc=s_ss)
            r_ms = kb.op("dve", lambda e: e.tensor_scalar(out=mseq, in0=ssq, scalar1=1.0 / 64.0, scalar2=EPS,
                                                          op0=ALU.mult, op1=ALU.add),
                         waits=[r_ss, (s_rs, nq)], inc=s_ms)
            r_rs = kb.op("pool", lambda e: e.tensor_tensor(out=rsq, in0=mseq, in1=c.nh[:, 0:8], op=ALU.pow),
                         waits=[r_ms, c.cst_ready, (s_xn, nq)], inc=s_rs)
            r_xn = kb.op("dve", lambda e: e.tensor_tensor(out=xn, in0=xf, in1=rsq.unsqueeze(2).to_broadcast([128, 8, 64]),
                                                          op=ALU.mult),
                         waits=[r_rs, r_xf, (s_xg, nq)], inc=s_xn)
            gb = gq_bc[:, which, :].unsqueeze(1).to_broadcast([128, 8, 64])
            r_xg = kb.op("dve", lambda e, gb=gb: e.tensor_tensor(out=xg, in0=xn, in1=gb, op=ALU.mult),
                         waits=[r_xn, c_ready, (s_t1, nq), (s_t2, 2 * nq)], inc=s_xg)
            cb = csb[:, 0, :].unsqueeze(1).to_broadcast([128, 8, 64])
            r_t1 = kb.op("pool", lambda e, cb=cb: e.tensor_tensor(out=t1, in0=xg, in1=cb, op=ALU.mult),
                         waits=[r_xg, r_cs, (s_qb, nq)], inc=s_t1)
            sb0 = csb[:, 1, 0:32].unsqueeze(1).to_broadcast([128, 8, 32])
            sb1 = csb[:, 1, 32:64].unsqueeze(1).to_broadcast([128, 8, 32])
            kb.op("pool", lambda e, sb0=sb0: e.tensor_tensor(out=t2[:, :, 0:32], in0=xg[:, :, 32:64], in1=sb0, op=ALU.mult),
                  waits=[r_xg, (s_qb, nq)], inc=s_t2)
            r_t2 = kb.op("pool", lambda e, sb1=sb1: e.tensor_tensor(out=t2[:, :, 32:64], in0=xg[:, :, 0:32], in1=sb1,
                                                                     op=ALU.mult), inc=s_t2)
            qb = qkb[nq % 2]
            r_qb = kb.op("dve", lambda e, qb=qb: e.tensor_tensor(out=qb, in0=t1, in1=t2, op=ALU.add),
                         waits=[r_t1, r_t2, (s_tq, nq - 1)], inc=s_qb)
            qbf = qb.rearrange("p h d -> p (h d)")
            for hp in range(4):
                r_tq = kb.op("pe", lambda e, hp=hp, qbf=qbf: e.transpose(out=tq[:, hp, :], in_=qbf[:, hp * 128:(hp + 1) * 128],
                                                                         identity=c.ident),
                             waits=[r_qb, (s_tqc, nq)] if hp == 0 else [], inc=s_tq if hp == 3 else None)
            if which == 0:
                kb.op("act", lambda e: e.activation(out=qz[0][0:64, :, b * 128:(b + 1) * 128], in_=tq[0:64, :, :],
                                                    func=AF.Copy), waits=[r_tq, i_ready])
                kb.op("act", lambda e: e.activation(out=qz[1][64:128, :, b * 128:(b + 1) * 128], in_=tq[64:128, :, :],
                                                    func=AF.Copy), inc=s_tqc)
            else:
                dst = kT[:, :, PAD + b * 128:PAD + (b + 1) * 128]
                kb.op("act", lambda e, dst=dst: e.activation(out=dst, in_=tq, func=AF.Copy), waits=[r_tq, i_ready], inc=s_tqc)
    for b in range(NB):
        qkv_block(b, 2 * b)
    kb.wait_only("sp", [(sm, sm.v) for sm in s_vst])


def e1a_pipelined(c, xd, layer):
    kb, A, I = c.kb, c.A, c.I
    j = layer // 2
    NB = S // 128
    kb.eng_sems()
    win_src = c.Wb["even_w_in"][j].rearrange("(c p) f -> p c f", p=128)
    g = norm_setup(c, kb, A, I["mix_norm"][layer], nx=3)
    hb = [TB(A.bf16([D])) for _ in range(2)]
    hT = [TB(A.bf16([8, 128])) for _ in range(2)]
    win = TB(A.bf16([8, 1024]))
    ub = [TB(A.bf16([4, 128])) for _ in range(5)]
    gvf = [TB(A.f32([4, 128])) for _ in range(3)]
    sqv = [TB(A.f32([4, 128])) for _ in range(2)]
    gv = [TB(A.bf16([4, 128])) for _ in range(2)]
    ao = [TB(A.bf16([4, 128])) for _ in range(2)]
    aost = [TB(A.bf16([4, 128])) for _ in range(2)]
    ssv, msev = TB(A.f32([4])), TB(A.f32([4]))
    rsvs = [TB(A.f32([4])) for _ in range(2)]
    wsp_f = TB(A.f32([4, 128]))
    wsp_b = TB(A.bf16([4, 128]))
    wspT = TB(A.bf16([4, 128]))
    sgn = TB(A.f32([4, 128]))
    bspT = TB(A.f32([4]))
    s_c = kb.sem("e1c")
    s_ast = [kb.sem("ast") for _ in range(2)]
    kb.dma("sp", win.ap[:, :, 0:512], win_src[:, :, 0:512], waits=[c.wready[c.pi]], inc=s_c)
    kb.dma("sp", win.ap[:, :, 512:1024], win_src[:, :, 512:1024], inc=s_c)
    kb.dma("sp", wsp_f.ap, I["sgu_w_spatial"][j].rearrange("g t s -> t g s"), inc=s_c)
    kb.dma("sp", sgn.ap.rearrange("p g t -> p (g t)"),
           I["sgu_norm"][j].rearrange("g t -> (g t)").partition_broadcast(128), inc=s_c)
    r_c = kb.op("sp", lambda e: e.dma_start(out=bspT.ap, in_=I["sgu_b_spatial"][j].rearrange("g t -> t g"),
                                            allow_slow_non_contiguous=True), inc=s_c, n=16)
    for tb_ in (win, wsp_f, sgn, bspT):
        tb_.w = r_c
    tp = TB(c.bank(0).bitcast(BF16)[:, 0:1024].rearrange("p (a b) -> p a b", a=8, b=128))
    zu, zv = TB(c.bank(1)), TB(c.bank(2))
    mxs = [TB(c.bank(bk).rearrange("p (a b) -> p a b", a=4, b=128)) for bk in (3, 5)]
    tas = [TB(c.bank(bk).bitcast(BF16)[:, 0:512].rearrange("p (a b) -> p a b", a=4, b=128)) for bk in (4, 6)]
    tpw = TB(c.bank(7).bitcast(BF16)[:, 0:512].rearrange("p (a b) -> p a b", a=4, b=128))
    kb.x("dve", lambda e: e.tensor_copy(out=wsp_b.ap, in_=wsp_f.ap), reads=[wsp_f], writes=[wsp_b])
    kb.x("pe", [lambda e, gg=gg: e.transpose(out=tpw.ap[:, gg, :], in_=wsp_b.ap[:, gg, :], identity=c.ident) for gg in range(4)],
         reads=[wsp_b], writes=[tpw], waits=[c.const_ready])
    kb.x("act", lambda e: e.activation(out=wspT.ap, in_=tpw.ap, func=AF.Copy), reads=[tpw], writes=[wspT])
    aoT_dst = c.aoT_d.rearrange("(g d) t -> d g t", g=4, d=128)

    def sl(b):
        nx = len(g["x"])
        xb = g["x"][b % nx]
        kb.xdma("sp", xb.ap, xd[b * 128:(b + 1) * 128, :], g["xl"][b % nx], writes=[xb])

    def s0a(b):
        nx = len(g["x"])
        norm_ops(c, kb, g, b, g["x"][b % nx], hb[b % 2])

    def s0b(b):
        h = hb[b % 2]
        kb.x("pe", [lambda e, dc=dc: e.transpose(out=tp.ap[:, dc, :], in_=h.ap[:, dc * 128:(dc + 1) * 128], identity=c.ident)
                    for dc in range(8)], reads=[h], writes=[tp], waits=[c.const_ready])
        kb.x("act", lambda e: e.activation(out=hT[b % 2].ap, in_=tp.ap, func=AF.Copy), reads=[tp], writes=[hT[b % 2]])

    def s1(b):
        hTb = hT[b % 2]
        kb.x("pe", [lambda e, dc=dc: e.matmul(zu.ap, hTb.ap[:, dc, :], win.ap[:, dc, 0:512], start=(dc == 0), stop=(dc == 7))
                    for dc in range(8)], reads=[hTb, win], writes=[zu])
        kb.x("act", lambda e: e.activation(out=ub[b % 5].ap.rearrange("p g d -> p (g d)"), in_=zu.ap, func=AF.Gelu),
             reads=[zu], writes=[ub[b % 5]])
        kb.x("pe", [lambda e, dc=dc: e.matmul(zv.ap, hTb.ap[:, dc, :], win.ap[:, dc, 512:1024], start=(dc == 0), stop=(dc == 7))
                    for dc in range(8)], reads=[hTb, win], writes=[zv])
        gf, sv = gvf[b % 3], sqv[b % 2]
        kb.x("act", lambda e: e.activation(out=gf.ap.rearrange("p g d -> p (g d)"), in_=zv.ap, func=AF.Gelu),
             reads=[zv], writes=[gf])
        kb.x("act", lambda e: e.activation(out=sv.ap, in_=gf.ap, func=AF.Square), reads=[gf], writes=[sv])

    def s2a(b):
        sv, rsv = sqv[b % 2], rsvs[b % 2]
        kb.x("dve", lambda e: e.tensor_reduce(out=ssv.ap, in_=sv.ap, axis=AX.X, op=ALU.add), reads=[sv], writes=[ssv])
        kb.x("dve", lambda e: e.tensor_scalar(out=msev.ap, in0=ssv.ap, scalar1=1.0 / 128.0, scalar2=EPS,
                                              op0=ALU.mult, op1=ALU.add), reads=[ssv], writes=[msev])
        kb.x("pool", lambda e: e.tensor_tensor(out=rsv.ap, in0=msev.ap, in1=c.nh[:, 0:4], op=ALU.pow),
             reads=[msev], writes=[rsv], waits=[c.cst_ready])

    def s2b(b):
        gf, gvb, rsv = gvf[b % 3], gv[b % 2], rsvs[b % 2]
        kb.x("dve", [lambda e, gg=gg: e.scalar_tensor_tensor(out=gvb.ap[:, gg, :], in0=gf.ap[:, gg, :],
                                                             scalar=rsv.ap[:, gg:gg + 1], in1=sgn.ap[:, gg, :],
                                                             op0=ALU.mult, op1=ALU.mult) for gg in range(4)],
             reads=[gf, rsv, sgn], writes=[gvb])

    def s3a(b):
        gvb, mx = gv[b % 2], mxs[b % 2]
        kb.x("pe", [lambda e, gg=gg: e.matmul(mx.ap[:, gg, :], wspT.ap[:, gg, :], gvb.ap[:, gg, :], start=True, stop=True)
                    for gg in range(4)], reads=[gvb, wspT], writes=[mx])

    def s3b(b):
        aob, ubb, mx = ao[b % 2], ub[b % 5], mxs[b % 2]
        kb.x("dve", [lambda e, gg=gg: e.scalar_tensor_tensor(out=aob.ap[:, gg, :], in0=mx.ap[:, gg, :],
                                                             scalar=bspT.ap[:, gg:gg + 1], in1=ubb.ap[:, gg, :],
                                                             op0=ALU.add, op1=ALU.mult) for gg in range(4)],
             reads=[mx, bspT, ubb], writes=[aob])

    def s4a(b):
        aob, ta = ao[b % 2], tas[b % 2]
        kb.x("pe", [lambda e, gg=gg: e.transpose(out=ta.ap[:, gg, :], in_=aob.ap[:, gg, :], identity=c.ident)
                    for gg in range(4)], reads=[aob], writes=[ta])

    def s4b(b):
        ta, stb = tas[b % 2], aost[b % 2]
        kb.x("act", lambda e: e.activation(out=stb.ap, in_=ta.ap, func=AF.Copy), reads=[ta], writes=[stb])
        kb.xdma("sp", aoT_dst[:, :, b * 128:(b + 1) * 128], stb.ap, s_ast[b % 2], reads=[stb])

    pipeline(NB, [sl, s0a, s0b, s1, s2a, s2b, s3a, s3b, s4a, s4b])
    kb.wait_only("sp", [(sm, sm.v) for sm in s_ast])


def e3_pipelined(c, xd, layer):
    kb, A, I = c.kb, c.A, c.I
    j = layer // 2
    NB = S // 128
    kb.eng_sems()
    wout = TB(A.bf16([8, D]))
    o3 = [[TB(A.f32([8, OH])) for _ in range(3)] for _ in range(3)]
    s_ol = [kb.sem("ol") for _ in range(3)]
    xs = [TB(A.f32([D])) for _ in range(4)]
    s_xl = [kb.sem("xl") for _ in range(4)]
    s_xs = [kb.sem("xs") for _ in range(4)]
    at = [TB(A.bf16([4, 128])) for _ in range(4)]
    s_al = [kb.sem("al") for _ in range(4)]
    rec = TB(A.f32([8]))
    bo = [TB(A.bf16([8, 64])) for _ in range(2)]
    boT = [TB(A.bf16([4, 128])) for _ in range(2)]
    s_c = kb.sem("e3c")
    wsrc = c.Wb["even_w_out"][j].rearrange("(c p) f -> p c f", p=128)
    kb.dma("sp", wout.ap[:, 0:4, :], wsrc[:, 0:4, :], waits=[c.wready[c.pi]], inc=s_c)
    kb.xdma("sp", wout.ap[:, 4:8, :], wsrc[:, 4:8, :], s_c, writes=[wout])
    tb = TB(c.bank(0).bitcast(BF16)[:, 0:512].rearrange("p (a b) -> p a b", a=4, b=128))
    yps = [TB(c.ps[:, 512:1536]), TB(c.ps[:, 1536:2560])]
    aoT_src = c.aoT_d.rearrange("(g d) t -> d g t", g=4, d=128)

    def sl(b):
        ob3 = o3[b % 3]
        for bi in range(3):
            kb.xdma("sp", ob3[bi].ap.rearrange("p h d -> p (h d)"), c.o_d[bi][b * 128:(b + 1) * 128, :], s_ol[b % 3],
                    writes=[ob3[bi]])
        for bi in range(3):
            ob3[bi].w = ob3[2].w
        kb.xdma("sp", xs[b % 4].ap, xd[b * 128:(b + 1) * 128, :], s_xl[b % 4], writes=[xs[b % 4]])
        kb.xdma("sp", at[b % 4].ap, aoT_src[:, :, b * 128:(b + 1) * 128], s_al[b % 4], writes=[at[b % 4]])

    def s0(b):
        ob3 = o3[b % 3]
        o0, o1, o2 = ob3
        kb.x("dve", lambda e: e.tensor_tensor(out=o0.ap, in0=o0.ap, in1=o1.ap, op=ALU.add), reads=[o1], writes=[o0])
        kb.x("dve", lambda e: e.tensor_tensor(out=o0.ap, in0=o0.ap, in1=o2.ap, op=ALU.add), reads=[o2], writes=[o0])
        kb.x("dve", lambda e: e.reciprocal(out=rec.ap, in_=o0.ap[:, :, 64]), reads=[o0], writes=[rec])
        bb = bo[b % 2]
        kb.x("dve", lambda e: e.tensor_tensor(out=bb.ap, in0=o0.ap[:, :, 0:64],
                                              in1=rec.ap.unsqueeze(2).to_broadcast([128, 8, 64]), op=ALU.mult),
             reads=[o0, rec], writes=[bb])

    def s1(b):
        bb, bt = bo[b % 2], boT[b % 2]
        bbf = bb.ap.rearrange("p h d -> p (h d)")
        kb.x("pe", [lambda e, hp=hp: e.transpose(out=tb.ap[:, hp, :], in_=bbf[:, hp * 128:(hp + 1) * 128], identity=c.ident)
                    for hp in range(4)], reads=[bb], writes=[tb], waits=[c.const_ready])
        kb.x("act", lambda e: e.activation(out=bt.ap, in_=tb.ap, func=AF.Copy), reads=[tb], writes=[bt])

    def s2(b):
        bt, a_, yp, xb = boT[b % 2], at[b % 4], yps[b % 2], xs[b % 4]
        fns = []
        for dh in range(2):
            for mc in range(8):
                lhsT = a_.ap[:, mc, :] if mc < 4 else bt.ap[:, mc - 4, :]
                fns.append(lambda e, dh=dh, mc=mc, lhsT=lhsT: e.matmul(
                    yp.ap[:, dh * 512:(dh + 1) * 512], lhsT, wout.ap[:, mc, dh * 512:(dh + 1) * 512],
                    start=(mc == 0), stop=(mc == 7)))
        kb.x("pe", fns, reads=[bt, a_, wout], writes=[yp])
        kb.x("dve", [lambda e, dh=dh: e.tensor_tensor(out=xb.ap[:, dh * 512:(dh + 1) * 512],
                                                      in0=yp.ap[:, dh * 512:(dh + 1) * 512],
                                                      in1=xb.ap[:, dh * 512:(dh + 1) * 512], op=ALU.add) for dh in range(2)],
             reads=[yp], writes=[xb])
        kb.xdma("sp", xd[b * 128:(b + 1) * 128, :], xb.ap, s_xs[b % 4], reads=[xb])

    pipeline(NB, [sl, s0, s1, s2])
    kb.wait_only("sp", [(sm, sm.v) for sm in s_xs])


def even_attn(c, xd, layer, aoT):
    kb, A, I = c.kb, c.A, c.I
    j = layer // 2
    NB = S // 128
    PAD = 1024
    A.reset()
    A.push()
    qz = [A.bf16([4, S]), A.bf16([4, S])]
    kT = A.bf16([4, S + 2 * PAD])
    A.freeze()
    win_src = c.Wb["even_w_in"][j].rearrange("(c p) f -> p c f", p=128)

    A.reset()
    if not _os.environ.get("KOLD"):
        e1b_pipelined(c, xd, layer, qz, kT, PAD)
    else:
        e1b_old(c, xd, layer, qz, kT, PAD, win_src)
    barrier(c)
    if DEBUG_STOP == "e1b":
        A.pop()
        return

    A.reset()
    vring = Ring([A.bf16([8, VH]) for _ in range(7)])
    s_vl = [kb.sem("vl") for _ in range(7)]
    pT = [A.bf16([2, 2, 128]) for _ in range(5)]
    NPT = 5
    osb = [A.f32([8, OH]) for _ in range(2)]
    s_ost = [kb.sem("ost") for _ in range(2)]
    s_sc, s_ex, s_mk, s_pv, s_oc = (kb.sem(n) for n in ("sc", "ex", "mk", "pv", "oc"))
    sps = [c.bank(bk).rearrange("p (h a q) -> p h a q", h=2, a=2, q=128) for bk in (0, 1, 6, 7)]
    NSP = len(sps)
    ops_ = [(c.bank(2)[:, 0:4 * OH].rearrange("p (h d) -> p h d", h=4, d=OH),
             c.bank(3)[:, 0:4 * OH].rearrange("p (h d) -> p h d", h=4, d=OH)),
            (c.bank(4)[:, 0:4 * OH].rearrange("p (h d) -> p h d", h=4, d=OH),
             c.bank(5)[:, 0:4 * OH].rearrange("p (h d) -> p h d", h=4, d=OH))]
    mask4 = c.mask2.rearrange("p (h a q) -> p h a q", h=2, a=2, q=128)

    units = []
    for bi, dil in enumerate((1, 4, 16)):
        NJ = S // dil // 128
        for r in range(dil):
            for jq in range(NJ):
                for hp in range(4):
                    units.append((bi, dil, r, jq, hp, NJ))
    if _os.environ.get("KE2N"):
        units = units[:int(_os.environ["KE2N"])]
    vt = {}

    def load_v(bi, dil, r, t):
        vi, slot, vb, w = vring.acquire()
        row0 = PAD + r + dil * (128 * t - 64)
        src = c.vaug_d[row0:row0 + 128 * dil, :].rearrange("(i d) f -> i d f", d=dil)[:, 0, :]
        rr = kb.dma("sp", vb.rearrange("p h d -> p (h d)"), src, waits=w, inc=s_vl[slot])
        vt[(bi, r, t)] = (vi, vb, rr)

    HB = 0 if _os.environ.get("KE2") == "hh0" else 64

    def scores(u):
        bi, dil, r, jq, hp, NJ = units[u]
        sp = sps[u % NSP]
        q0 = r + dil * 128 * jq
        n = 0
        for hh in range(2):
            for ab in range(2):
                k0 = PAD + r + dil * (128 * (jq + ab) - 64)
                n += 1
                rr = kb.op("pe", lambda e, sp=sp, hh=hh, ab=ab, hp=hp, k0=k0, q0=q0, dil=dil: e.matmul(
                    sp[:, hh, ab, :], kT[:, hp, k0:k0 + 127 * dil + 1:dil],
                    qz[hh][:, hp, q0:q0 + 127 * dil + 1:dil], start=True, stop=True),
                    waits=[(s_ex, u - NSP + 1)] if n == 1 else [], inc=s_sc if n == 4 else None)
        return rr

    def softmax_pv(u, r_sc):
        bi, dil, r, jq, hp, NJ = units[u]
        sp = sps[u % NSP]
        pt = pT[u % NPT]
        r_ex = kb.op("act", lambda e, sp=sp, pt=pt: e.activation(out=pt, in_=sp, func=AF.Exp, scale=0.125),
                     waits=[r_sc, (s_pv, u - NPT + 1)], inc=s_ex)
        r_mk = kb.op("dve", lambda e, pt=pt: e.tensor_tensor(out=pt, in0=pt, in1=mask4, op=ALU.mult),
                     waits=[r_ex, c.const_ready], inc=s_mk)
        blk = u // 4
        olo, ohi = ops_[blk % 2]
        n = 0
        for hh in range(2):
            h = hp * 2 + hh
            ot = (olo if h < 4 else ohi)[:, h % 4, :]
            for ab in range(2):
                vi, vb, vr = vt[(bi, r, jq + ab)]
                n += 1
                rr = kb.op("pe", lambda e, ot=ot, pt=pt, hh=hh, ab=ab, vb=vb, h=h: e.matmul(
                    ot, pt[:, hh, ab, :], vb[:, h, 0:OH], start=(ab == 0), stop=(ab == 1)),
                    waits=[r_mk, vr, vt[(bi, r, jq + 1)][2], (s_oc, 2 * (blk - 1))] if n == 1 else [],
                    inc=s_pv if n == 4 else None)
        return rr

    pend = []
    LA = NSP - 1

    def finish(pu, pr):
        r_pv = softmax_pv(pu, pr)
        _attn_block_end(c, kb, units, pu, r_pv, vt, vring, ops_, osb, s_oc, s_ost)

    for u in range(len(units)):
        bi, dil, r, jq, hp, NJ = units[u]
        if hp == 0:
            if u // 4 < 16:
                c.cast_pump(1)
            if jq == 0:
                load_v(bi, dil, r, 0)
                load_v(bi, dil, r, 1)
            if jq + 2 <= NJ:
                load_v(bi, dil, r, jq + 2)
        pend.append((u, scores(u)))
        if len(pend) > LA:
            finish(*pend.pop(0))
    while pend:
        finish(*pend.pop(0))
    kb.wait_only("sp", [(sm, sm.v) for sm in s_ost])
    barrier(c)
    A.pop()
    if DEBUG_STOP == "e2":
        return
    c.A.reset()
    e3_pipelined(c, xd, layer)


def _attn_block_end(c, kb, units, u, r_pv, vt, vring, ops_, osb, s_oc, s_ost):
    bi, dil, r, jq, hp, NJ = units[u]
    if hp != 3:
        return
    blk = u // 4
    olo, ohi = ops_[blk % 2]
    ob = osb[blk % 2]
    wst = [(s_ost[blk % 2], 16 * (blk // 2))] if blk >= 2 else []
    kb.op("dve", lambda e: e.tensor_copy(out=ob[:, 0:4, :], in_=olo), waits=[r_pv] + wst, inc=s_oc)
    r_oc = kb.op("dve", lambda e: e.tensor_copy(out=ob[:, 4:8, :], in_=ohi), inc=s_oc)
    row0 = r + dil * 128 * jq
    dst = c.o_d[bi][row0:row0 + 128 * dil, :].rearrange("(i d) f -> i d f", d=dil)[:, 0, :]
    kb.dma("sp", dst, ob.rearrange("p h d -> p (h d)"), waits=[r_oc], inc=s_ost[blk % 2])
    vring.release(vt[(bi, r, jq)][0], [r_pv])
    if jq == NJ - 1:
        vring.release(vt[(bi, r, jq + 1)][0], [r_pv])


def even_out(c, xd, layer, aoT):
    kb, A, I = c.kb, c.A, c.I
    j = layer // 2
    NB = S // 128
    A.reset()
    wout = A.bf16([8, D])
    oring = Ring([[A.f32([8, OH]) for _ in range(3)] for _ in range(2)])
    s_ol = [kb.sem("ol") for _ in range(2)]
    xring = Ring([A.f32([D]) for _ in range(3)])
    s_xl = [kb.sem("xl") for _ in range(3)]
    s_xs = [kb.sem("xs") for _ in range(3)]
    rec = A.f32([8])
    bo = [A.bf16([8, 64]) for _ in range(2)]
    boT = [A.bf16([4, 128]) for _ in range(2)]
    s_c = kb.sem("e3c")
    wsrc = c.Wb["even_w_out"][j].rearrange("(c p) f -> p c f", p=128)
    kb.dma("sp", wout[:, 0:4, :], wsrc[:, 0:4, :], inc=s_c)
    kb.dma("sp", wout[:, 4:8, :], wsrc[:, 4:8, :], inc=s_c)
    c_ready = (s_c, s_c.v)
    s_a1, s_a2, s_rc, s_bo, s_tb, s_tbc, s_pj, s_ad = (kb.sem(n) for n in ("a1", "a2", "rc", "bo", "tb", "tbc", "pj", "ad"))
    tb = c.bank(0).bitcast(BF16)[:, 0:512].rearrange("p (a b) -> p a b", a=4, b=128)
    yps = [c.ps[:, 1024:2048], c.ps[:, 2048:3072]]

    def loads(b):
        oi, oslot, ob3, w = oring.acquire()
        for bi in range(3):
            r_o = kb.dma("sp", ob3[bi].rearrange("p h d -> p (h d)"), c.o_d[bi][b * 128:(b + 1) * 128, :],
                         waits=w if bi == 0 else [], inc=s_ol[oslot])
        xi, xslot, xb, wx = xring.acquire()
        r_x = kb.dma("sp", xb, xd[b * 128:(b + 1) * 128, :], waits=wx, inc=s_xl[xslot])
        return (oi, ob3, r_o, xi, xslot, xb, r_x)

    items = {0: loads(0)}
    for b in range(NB):
        if b + 1 < NB:
            items[b + 1] = loads(b + 1)
        oi, ob3, r_o, xi, xslot, xb, r_x = items[b]
        o0, o1, o2 = ob3
        r1 = kb.op("dve", lambda e, o0=o0, o1=o1: e.tensor_tensor(out=o0, in0=o0, in1=o1, op=ALU.add), waits=[r_o], inc=s_a1)
        r2 = kb.op("dve", lambda e, o0=o0, o2=o2: e.tensor_tensor(out=o0, in0=o0, in1=o2, op=ALU.add), waits=[r1], inc=s_a2)
        r3 = kb.op("dve", lambda e, o0=o0: e.reciprocal(out=rec, in_=o0[:, :, 64]), waits=[r2, (s_bo, b)], inc=s_rc)
        bb = bo[b % 2]
        r4 = kb.op("dve", lambda e, o0=o0, bb=bb: e.tensor_tensor(
            out=bb, in0=o0[:, :, 0:64], in1=rec.unsqueeze(2).to_broadcast([128, 8, 64]), op=ALU.mult),
            waits=[r3, (s_tb, b - 1)], inc=s_bo)
        oring.release(oi, [r4])
        bbf = bb.rearrange("p h d -> p (h d)")
        for hp in range(4):
            r5 = kb.op("pe", lambda e, hp=hp, bbf=bbf: e.transpose(out=tb[:, hp, :], in_=bbf[:, hp * 128:(hp + 1) * 128],
                                                                   identity=c.ident),
                       waits=[r4, c.const_ready, (s_tbc, b)] if hp == 0 else [], inc=s_tb if hp == 3 else None)
        bt = boT[b % 2]
        r6 = kb.op("act", lambda e, bt=bt: e.activation(out=bt, in_=tb, func=AF.Copy), waits=[r5, (s_pj, b - 1)], inc=s_tbc)
        yp = yps[b % 2]
        for dh in range(2):
            for mc in range(8):
                lhsT = aoT[:, mc, b * 128:(b + 1) * 128] if mc < 4 else bt[:, mc - 4, :]
                r7 = kb.op("pe", lambda e, dh=dh, mc=mc, lhsT=lhsT, yp=yp: e.matmul(
                    yp[:, dh * 512:(dh + 1) * 512], lhsT, wout[:, mc, dh * 512:(dh + 1) * 512],
                    start=(mc == 0), stop=(mc == 7)),
                    waits=[r6, c_ready, (s_ad, 2 * (b - 1))] if (dh == 0 and mc == 0) else [],
                    inc=s_pj if (dh == 1 and mc == 7) else None)
        for dh in range(2):
            xs = xb[:, dh * 512:(dh + 1) * 512]
            r8 = kb.op("dve", lambda e, xs=xs, dh=dh, yp=yp: e.tensor_tensor(out=xs, in0=yp[:, dh * 512:(dh + 1) * 512],
                                                                             in1=xs, op=ALU.add),
                       waits=[r7, r_x] if dh == 0 else [], inc=s_ad)
        r9 = kb.dma("sp", xd[b * 128:(b + 1) * 128, :], xb, waits=[r8], inc=s_xs[xslot])
        xring.release(xi, [r9])
    kb.wait_only("sp", [(sm, sm.v) for sm in s_xs])


_CACHE = {}


def _pool_consts():
    wins = (2, 4, 8, 16)
    out = np.zeros((128, 4, 5, 128), np.float32)
    for g, win in enumerate(wins):
        lo = win // 2
        hi = win - 1 - lo

        def m(t, s_):
            start = min(max(t - lo, 0), S)
            end = min(max(t + hi + 1, 0), S)
            v = 0.0
            if start <= s_ < end:
                v = 1.0 / float(end - start)
            if s_ == t:
                v -= 1.0
            return v
        for tl in range(128):
            for sl in range(128):
                out[sl, g, 0, tl] = m(512 + tl, 512 - 128 + sl)
                out[sl, g, 1, tl] = m(512 + tl, 512 + sl)
                out[sl, g, 2, tl] = m(512 + tl, 512 + 128 + sl)
                out[sl, g, 3, tl] = m(tl, sl)
                out[sl, g, 4, tl] = m(S - 128 + tl, S - 128 + sl)
    return out.astype(ml_dtypes.bfloat16)


def _consts():
    kk = np.arange(128)[:, None]
    qq = np.arange(128)[None, :]
    ma = (kk >= qq).astype(np.float32)
    mb = (kk <= qq).astype(np.float32)
    m2 = np.concatenate([ma, mb, ma, mb], axis=1)
    pos = np.arange(S, dtype=np.float32)
    inv_freq = (np.float32(10000.0) ** (-np.arange(0, 64, 2, dtype=np.float32) / np.float32(64))).astype(np.float32)
    ang = (pos[:, None] * inv_freq[None, :]).astype(np.float32)
    ang = np.concatenate([ang, ang], axis=-1)
    cos = np.cos(ang).astype(np.float32)
    sin = np.sin(ang).astype(np.float32)
    sin[:, :32] *= -1.0
    return {"c_ident": np.eye(128, dtype=np.float32).astype(ml_dtypes.bfloat16),
            "c_pool": _pool_consts(),
            "c_mask": m2.astype(ml_dtypes.bfloat16),
            "c_cos": np.ascontiguousarray(cos), "c_sin": np.ascontiguousarray(sin)}


def kernel(**inputs):
    nph = inputs.pop("_nphases", 99)
    skip = inputs.pop("_skip", 0)
    cores = inputs.pop("_cores", NCORES)
    trace = inputs.pop("_trace", False)
    key = (nph, skip)
    if key not in _CACHE:
        _CACHE[key] = build_program(nph, skip)
    nc = _CACHE[key]
    x = np.asarray(inputs["x"], dtype=np.float32)
    shared = {k: np.ascontiguousarray(np.asarray(v, dtype=np.float32)) for k, v in inputs.items() if k != "x"}
    if "consts" not in _CACHE:
        _CACHE["consts"] = _consts()
    shared.update(_CACHE["consts"])
    in_maps = []
    for ci in range(cores):
        m = dict(shared)
        m["x"] = np.ascontiguousarray(x[ci])
        in_maps.append(m)
    res = run_bass_kernel_spmd(nc, in_maps, core_ids=list(range(cores)), **({"trace": True} if trace else {}))
    if trace:
        print("EXEC_NS", res.exec_time_ns)
    return np.stack([np.asarray(r["out"], dtype=np.float32) for r in res.results], axis=0)
```
